# Optimizing a Trainium2 kernel written in Bass

```python
import math
import jax, jax.numpy as jnp
from jax import lax
import numpy as np

D_MODEL = 1024
BATCH = 8
SEQ = 2048
DEPTH = 2
DEC_BATCH = 128
DEC_SEQ = 8
PAST_LEN = 16384
PAGE_SIZE = 128

N_MIXERS = 2
N_GM_LAYERS = (DEPTH + 1) // 2
N_SSM_LAYERS = DEPTH // 2
GM_CHUNK = 128
D_GM = 2 * D_MODEL
GM_GROUPS = 8
GM_GROUP_W = D_GM // GM_GROUPS
D_INNER = 2 * D_MODEL
SSM_HEAD_DIM = 64
SSM_HEADS = D_INNER // SSM_HEAD_DIM
SSM_GROUPS = 8
HEADS_PER_GROUP = SSM_HEADS // SSM_GROUPS
D_STATE = 128
CONV_W = 4
CONV_DIM = D_INNER + 2 * SSM_GROUPS * D_STATE
D_IN_PROJ = D_INNER + CONV_DIM + SSM_HEADS
SSM_CHUNK = 128
D_FF = 4 * D_MODEL
N_MOD = 6
EPS = 1e-6

kernel_name = 'hybrid_chunkgmlp_mamba2_adaln_step'


def rms_norm(x, g):
    xf = x.astype(jnp.float32)
    y = xf * lax.rsqrt(jnp.mean(xf * xf, axis=-1, keepdims=True) + EPS)
    return (y * g.astype(jnp.float32)).astype(x.dtype)


def layer_norm(x, g, b):
    xf = x.astype(jnp.float32)
    mu = jnp.mean(xf, axis=-1, keepdims=True)
    xc = xf - mu
    y = xc * lax.rsqrt(jnp.mean(xc * xc, axis=-1, keepdims=True) + EPS)
    return (y * g.astype(jnp.float32) + b.astype(jnp.float32)).astype(x.dtype)


def modulate(h, shift, scale):
    return h * (1 + scale[:, None, :]) + shift[:, None, :]


def pad_len(a, lp):
    l = a.shape[1]
    if lp == l:
        return a
    return jnp.pad(a, [(0, 0), (0, lp - l)] + [(0, 0)] * (a.ndim - 2))


def chunk_gmlp_mixer(h, w_in, ln_g, ln_b, w_s, b_s, w_out):
    bsz, l, _ = h.shape
    z = jax.nn.gelu(h @ w_in)
    u = z[..., :D_GM]
    v = layer_norm(z[..., D_GM:], ln_g, ln_b)
    q = min(l, GM_CHUNK)
    n_chunks = -(-l // q)
    lp = n_chunks * q
    vc = pad_len(v, lp).reshape(bsz, n_chunks, q, GM_GROUPS, GM_GROUP_W)
    causal = jnp.tril(jnp.ones((q, q), dtype=bool))
    ws = jnp.where(causal[None], w_s[:, :q, :q], 0)
    s = jnp.einsum('gts,bcsgw->bctgw', ws, vc) + b_s[:, :q].T[:, :, None]
    s = s.reshape(bsz, lp, D_GM)[:, :l]
    y = (u * s) @ w_out
    start = ((l - 1) // GM_CHUNK) * GM_CHUNK
    return y, v[:, start:]


def ssd_scan(x, dt, a, bm, cm, state0):
    bsz, l = x.shape[:2]
    q = min(l, SSM_CHUNK)
    nc = -(-l // q)
    lp = nc * q
    f32 = jnp.float32
    xdt = pad_len(x.astype(f32) * dt[..., None], lp).reshape(bsz, nc, q, SSM_GROUPS, HEADS_PER_GROUP, SSM_HEAD_DIM)
    da = pad_len(dt * a, lp).reshape(bsz, nc, q, SSM_GROUPS, HEADS_PER_GROUP)
    bm = pad_len(bm.astype(f32), lp).reshape(bsz, nc, q, SSM_GROUPS, D_STATE)
    cm = pad_len(cm.astype(f32), lp).reshape(bsz, nc, q, SSM_GROUPS, D_STATE)
    acum = jnp.cumsum(da, axis=2)
    causal = jnp.tril(jnp.ones((q, q), dtype=bool))[:, :, None, None]
    seg = acum[:, :, :, None] - acum[:, :, None, :]
    decay = jnp.exp(jnp.where(causal, seg, -jnp.inf))
    cb = jnp.einsum('bctgn,bcsgn->bctsg', cm, bm)
    y_diag = jnp.einsum('bctsgr,bcsgrp->bctgrp', cb[..., None] * decay, xdt)
    decay_to_end = jnp.exp(acum[:, :, -1:] - acum)
    chunk_states = jnp.einsum('bcsgn,bcsgrp->bcgrpn', bm, xdt * decay_to_end[..., None])
    chunk_decay = jnp.exp(acum[:, :, -1])

    def step(carry, inp):
        cs, cd = inp
        return carry * cd[..., None, None] + cs, carry

    s0 = state0.astype(f32).reshape(bsz, SSM_GROUPS, HEADS_PER_GROUP, SSM_HEAD_DIM, D_STATE)
    final, entering = lax.scan(step, s0, (jnp.moveaxis(chunk_states, 1, 0), jnp.moveaxis(chunk_decay, 1, 0)))
    entering = jnp.moveaxis(entering, 0, 1)
    y_off = jnp.einsum('bctgn,bcgrpn->bctgrp', cm, entering) * jnp.exp(acum)[..., None]
    y = (y_diag + y_off).reshape(bsz, lp, SSM_HEADS, SSM_HEAD_DIM)[:, :l]
    return y, final.reshape(bsz, SSM_HEADS, SSM_HEAD_DIM, D_STATE)


def mamba2_mixer(h, conv_state, ssm_state, w_in, conv_w, conv_b, dt_bias, a_log, d_skip, norm_g, w_out):
    bsz, l, _ = h.shape
    zxbcdt = h @ w_in
    z = zxbcdt[..., :D_INNER]
    xbc = zxbcdt[..., D_INNER:D_INNER + CONV_DIM]
    dt_raw = zxbcdt[..., D_INNER + CONV_DIM:]
    xp = jnp.concatenate([conv_state.astype(xbc.dtype), xbc], axis=1)
    conv = conv_b + xp[:, 0:l] * conv_w[0]
    for k in range(1, CONV_W):
        conv = conv + xp[:, k:k + l] * conv_w[k]
    xbc_c = jax.nn.silu(conv)
    new_conv = xp[:, l:]
    xs = xbc_c[..., :D_INNER].reshape(bsz, l, SSM_HEADS, SSM_HEAD_DIM)
    bm = xbc_c[..., D_INNER:D_INNER + SSM_GROUPS * D_STATE].reshape(bsz, l, SSM_GROUPS, D_STATE)
    cm = xbc_c[..., D_INNER + SSM_GROUPS * D_STATE:].reshape(bsz, l, SSM_GROUPS, D_STATE)
    dt = jax.nn.softplus(dt_raw.astype(jnp.float32) + dt_bias.astype(jnp.float32))
    a = -jnp.exp(a_log.astype(jnp.float32))
    y, new_ssm = ssd_scan(xs, dt, a, bm, cm, ssm_state)
    y = y + xs.astype(jnp.float32) * d_skip.astype(jnp.float32)[:, None]
    y = y.reshape(bsz, l, D_INNER).astype(h.dtype) * jax.nn.silu(z)
    yg = rms_norm(y.reshape(bsz, l, SSM_GROUPS, D_INNER // SSM_GROUPS), norm_g.reshape(SSM_GROUPS, -1))
    out = yg.reshape(bsz, l, D_INNER) @ w_out
    return out, new_conv, new_ssm.astype(ssm_state.dtype)


def sq_relu_mlp(h, w1, w2):
    return jnp.square(jax.nn.relu(h @ w1)) @ w2


def trunk(x, c, ssm_states, conv_states, ada_w, ada_b, norm1_g, norm2_g,
          gm_w_in, gm_ln_g, gm_ln_b, gm_w_s, gm_b_s, gm_w_out,
          ssm_w_in, ssm_conv_w, ssm_conv_b, ssm_dt_bias, ssm_a_log, ssm_d, ssm_norm_g, ssm_w_out,
          mlp_w1, mlp_w2, final_g):
    new_v, new_conv, new_ssm = [], [], []
    for i in range(DEPTH):
        mod = (jax.nn.silu(c) @ ada_w[i] + ada_b[i]).reshape(c.shape[0], N_MOD, D_MODEL)
        h = modulate(rms_norm(x, norm1_g[i]), mod[:, 0], mod[:, 1])
        j = i // N_MIXERS
        if i % N_MIXERS == 0:
            out, v = chunk_gmlp_mixer(h, gm_w_in[j], gm_ln_g[j], gm_ln_b[j], gm_w_s[j], gm_b_s[j], gm_w_out[j])
            new_v.append(v)
        else:
            out, cs, ss = mamba2_mixer(h, conv_states[j], ssm_states[j], ssm_w_in[j], ssm_conv_w[j], ssm_conv_b[j],
                                       ssm_dt_bias[j], ssm_a_log[j], ssm_d[j], ssm_norm_g[j], ssm_w_out[j])
            new_conv.append(cs)
            new_ssm.append(ss)
        x = x + mod[:, 2][:, None, :] * out
        h = modulate(rms_norm(x, norm2_g[i]), mod[:, 3], mod[:, 4])
        x = x + mod[:, 5][:, None, :] * sq_relu_mlp(h, mlp_w1[i], mlp_w2[i])
    y = rms_norm(x, final_g)
    return y, jnp.stack(new_v), jnp.stack(new_ssm), jnp.stack(new_conv)


def setup_inputs(seed: int = 0) -> dict:
    key = jax.random.key(seed)
    ks = iter(jax.random.split(key, 40))
    f32 = jnp.float32

    def nrm(shape, scale):
        return jax.random.normal(next(ks), shape, f32) * scale

    NG, NS = N_GM_LAYERS, N_SSM_LAYERS
    dt0 = jnp.exp(jax.random.uniform(next(ks), (NS, SSM_HEADS), f32, math.log(1e-3), math.log(1e-1)))
    dt_bias = dt0 + jnp.log(-jnp.expm1(-dt0))
    a_log = jnp.log(jax.random.uniform(next(ks), (NS, SSM_HEADS), f32, 1.0, 16.0))
    return {
        'x_prompt': nrm((BATCH, SEQ, D_MODEL), 1.0),
        'x_sample': nrm((DEC_BATCH, DEC_SEQ, D_MODEL), 1.0),
        'c_prompt': nrm((BATCH, D_MODEL), 1.0),
        'c_sample': nrm((DEC_BATCH, D_MODEL), 1.0),
        'state_ssm': nrm((NS, DEC_BATCH, SSM_HEADS, SSM_HEAD_DIM, D_STATE), 0.1),
        'state_conv': nrm((NS, DEC_BATCH, CONV_W - 1, CONV_DIM), 1.0),
        'ada_w': nrm((DEPTH, D_MODEL, N_MOD * D_MODEL), D_MODEL ** -0.5),
        'ada_b': nrm((DEPTH, N_MOD * D_MODEL), 0.01),
        'norm1_g': 1.0 + nrm((DEPTH, D_MODEL), 0.02),
        'norm2_g': 1.0 + nrm((DEPTH, D_MODEL), 0.02),
        'gm_w_in': nrm((NG, D_MODEL, 2 * D_GM), D_MODEL ** -0.5),
        'gm_ln_g': 1.0 + nrm((NG, D_GM), 0.02),
        'gm_ln_b': nrm((NG, D_GM), 0.02),
        'gm_w_s': nrm((NG, GM_GROUPS, GM_CHUNK, GM_CHUNK), GM_CHUNK ** -0.5),
        'gm_b_s': 1.0 + nrm((NG, GM_GROUPS, GM_CHUNK), 0.1),
        'gm_w_out': nrm((NG, D_GM, D_MODEL), D_GM ** -0.5),
        'ssm_w_in': nrm((NS, D_MODEL, D_IN_PROJ), D_MODEL ** -0.5),
        'ssm_conv_w': nrm((NS, CONV_W, CONV_DIM), CONV_W ** -0.5),
        'ssm_conv_b': nrm((NS, CONV_DIM), 0.02),
        'ssm_dt_bias': dt_bias,
        'ssm_a_log': a_log,
        'ssm_d': 1.0 + nrm((NS, SSM_HEADS), 0.1),
        'ssm_norm_g': 1.0 + nrm((NS, D_INNER), 0.02),
        'ssm_w_out': nrm((NS, D_INNER, D_MODEL), D_INNER ** -0.5),
        'mlp_w1': nrm((DEPTH, D_MODEL, D_FF), D_MODEL ** -0.5),
        'mlp_w2': nrm((DEPTH, D_FF, D_MODEL), D_FF ** -0.5),
        'final_g': 1.0 + nrm((D_MODEL,), 0.02),
    }


def reference(x_prompt, x_sample, c_prompt, c_sample, state_ssm, state_conv, ada_w, ada_b, norm1_g, norm2_g,
              gm_w_in, gm_ln_g, gm_ln_b, gm_w_s, gm_b_s, gm_w_out,
              ssm_w_in, ssm_conv_w, ssm_conv_b, ssm_dt_bias, ssm_a_log, ssm_d, ssm_norm_g, ssm_w_out,
              mlp_w1, mlp_w2, final_g):
    weights = (ada_w, ada_b, norm1_g, norm2_g, gm_w_in, gm_ln_g, gm_ln_b, gm_w_s, gm_b_s, gm_w_out,
               ssm_w_in, ssm_conv_w, ssm_conv_b, ssm_dt_bias, ssm_a_log, ssm_d, ssm_norm_g, ssm_w_out,
               mlp_w1, mlp_w2, final_g)
    bp = x_prompt.shape[0]
    ssm0 = jnp.zeros((N_SSM_LAYERS, bp, SSM_HEADS, SSM_HEAD_DIM, D_STATE), x_prompt.dtype)
    conv0 = jnp.zeros((N_SSM_LAYERS, bp, CONV_W - 1, CONV_DIM), x_prompt.dtype)
    y_prompt, gm_v_prompt, ssm_state_prompt, conv_state_prompt = trunk(x_prompt, c_prompt, ssm0, conv0, *weights)
    y_sample, gm_v_sample, ssm_state_sample, conv_state_sample = trunk(x_sample, c_sample, state_ssm, state_conv, *weights)
    return (y_prompt, y_sample, gm_v_prompt, gm_v_sample, ssm_state_prompt, conv_state_prompt,
            ssm_state_sample, conv_state_sample)
```

```python
import math
import numpy as np
import concourse.bass as bass
import concourse.mybir as mybir
from concourse.bass_utils import run_bass_kernel_spmd

F32 = mybir.dt.float32
BF16 = mybir.dt.bfloat16
AF = mybir.ActivationFunctionType
ALU = mybir.AluOpType

NCORES = 8
D = 1024
KC = 8
SEQ = 2048
NSAMP = 16
LS = 8
DGM = 2048
DFF = 4096
DIN = 2048
NH = 32
HD = 64
NG = 8
DST = 128
CONVD = 4096
EPS = 1e-6
NTOK = SEQ + NSAMP * LS

SL_GV, SL_GU, SL_GO, SL_M1_0, SL_M2_0, SL_SI, SL_SO, SL_M1_1, SL_M2_1 = 0, 4, 8, 12, 20, 28, 40, 44, 52
NSLAB = 60
SLAB_GROUPS = [(0, 4), (4, 8), (8, 12), (12, 20), (20, 28), (28, 40), (40, 44), (44, 52), (52, 60)]

PF_ADAB, PF_N1G, PF_N2G, PF_FG, PF_LNG, PF_LNB, PF_CW, PF_CB, PF_SNG, PF_D = 0, 96, 112, 128, 136, 152, 168, 296, 328, 344
PF_N = 360
C_ID, C_LT, C_U, C_LTB, C_UB, C_SAME, C_ONES, C_KILL, C_KILLB, C_NEGI, C_SEQIND = 0, 128, 256, 384, 512, 640, 768, 896, 1024, 1152, 1280
C_N = 1296


class Res:
    __slots__ = ("name", "w", "r")

    def __init__(self, name):
        self.name = name
        self.w = None
        self.r = {}


class DSem:
    def __init__(self, handle):
        self.handle = handle
        self.total = 0


class KB:
    def __init__(self):
        self.nc = bass.Bass("TRN2", target_bir_lowering=False)
        nc = self.nc
        self.dry = False
        self.eng = {"pe": nc.tensor, "act": nc.scalar, "dve": nc.vector, "pool": nc.gpsimd, "sp": nc.sync}
        self.sem = {e: nc.alloc_semaphore("s_" + e) for e in ("pe", "act", "dve", "pool")}
        self.cnt = {e: 0 for e in ("pe", "act", "dve", "pool")}
        self.seen = {e: {} for e in self.eng}
        self.snap = {e: {} for e in ("pe", "act", "dve", "pool")}
        self._sb = {}
        self._res = {}
        self._ds = {}
        self.out_sems = []

    def sb(self, name, shape, dtype):
        if name not in self._sb:
            self._sb[name] = self.nc.alloc_sbuf_tensor("sb_" + name, list(shape), dtype).ap()
        return self._sb[name]

    def res(self, name):
        if name not in self._res:
            self._res[name] = Res(name)
        return self._res[name]

    def dsem(self, name):
        if name not in self._ds:
            self._ds[name] = DSem(self.nc.alloc_semaphore("d_" + name))
        return self._ds[name]

    def _wait(self, eng, key, val):
        if key == "pe" and eng == "pe":
            return
        if self.seen[eng].get(key, 0) >= val:
            return
        h = self.sem[key] if isinstance(key, str) else key.handle
        self.eng[eng].wait_ge(h, val)
        self._learn(eng, key, val)

    def _learn(self, eng, key, val):
        se = self.seen[eng]
        if se.get(key, 0) < val:
            se[key] = val
        if isinstance(key, str):
            sn = self.snap[key].get(val)
            if sn:
                for k2, v2 in sn.items():
                    if se.get(k2, 0) < v2:
                        se[k2] = v2

    def _deps(self, eng, reads, writes, lhs=None):
        need = {}
        for r in reads:
            if r.w is not None:
                k, v = r.w
                if need.get(k, 0) < v:
                    need[k] = v
        for w in writes:
            if w.w is not None:
                k, v = w.w
                if need.get(k, 0) < v:
                    need[k] = v
            for k, v in w.r.items():
                if need.get(k, 0) < v:
                    need[k] = v
        todo = []
        for k, v in need.items():
            if k == "pe" and eng == "pe":
                continue
            if self.seen[eng].get(k, 0) >= v:
                continue
            todo.append((k, v))
        attach = None
        if todo and eng != "pe":
            attach = todo.pop()
        elif todo and lhs is not None:
            lneed = {}
            for r in lhs:
                if r.w is not None:
                    k, v = r.w
                    if lneed.get(k, 0) < v:
                        lneed[k] = v
            for i in range(len(todo) - 1, -1, -1):
                k, v = todo[i]
                if lneed.get(k, 0) <= self.seen[eng].get(k, 0):
                    attach = todo.pop(i)
                    break
        for k, v in todo:
            self._wait(eng, k, v)
        return attach

    def _attach(self, eng, ins, attach):
        if attach is not None:
            k, v = attach
            h = self.sem[k] if isinstance(k, str) else k.handle
            ins._wait_ge(h, v)
            self._learn(eng, k, v)

    def op(self, eng, fn, reads=(), writes=(), lhs=None):
        if self.dry:
            return
        attach = self._deps(eng, reads, writes, lhs)
        ins = fn()
        self._attach(eng, ins, attach)
        self.cnt[eng] += 1
        n = self.cnt[eng]
        ins.then_inc(self.sem[eng], 1)
        self.snap[eng][n] = dict(self.seen[eng])
        for r in reads:
            r.r[eng] = n
        for w in writes:
            w.w = (eng, n)
            w.r = {}

    def dma(self, q, out, in_, ds, reads=(), writes=(), is_out=False, **kw):
        if self.dry:
            return
        if is_out and ds not in self.out_sems:
            self.out_sems.append(ds)
        attach = self._deps(q, reads, writes)
        ins = self.eng[q].dma_start(out=out, in_=in_, **kw)
        self._attach(q, ins, attach)
        ds.total += 16
        ins.then_inc(ds.handle, 16)
        for r in reads:
            r.r[ds] = ds.total
        for w in writes:
            w.w = (ds, ds.total)
            w.r = {}

    def fence_all(self, resources, into):
        acc = {}
        for r in resources:
            if r.w is not None:
                k, v = r.w
                acc[k] = max(acc.get(k, 0), v)
            for k, v in r.r.items():
                acc[k] = max(acc.get(k, 0), v)
        for r in into:
            for k, v in acc.items():
                r.r[k] = max(r.r.get(k, 0), v)


class WStream:
    def __init__(self, kb, nslots, depth, wsc, wsc_res, ada):
        self.kb = kb
        self.nslots = nslots
        self.depth = depth
        self.wsc = wsc
        self.wsc_res = wsc_res
        self.ada = ada
        self.slots = [kb.sb("wslot%d" % i, [128, 4096], BF16) for i in range(nslots)]
        self.sres = [kb.res("wslot%d" % i) for i in range(nslots)]
        self.sds = [kb.dsem("wslot%d" % i) for i in range(nslots)]
        self.plan = []
        self.pos = 0
        self.issued = 0

    def start_real(self):
        self.pos = 0
        self.issued = 0
        self.released = 0

    def _issue(self, i):
        kb = self.kb
        kind, idx = self.plan[i]
        s = i % self.nslots
        if kind == "sc":
            assert self.wsc_res[idx].w is not None, ("weight slab loaded before its cast was emitted", idx)
            kb.dma("sp", self.slots[s][:], self.wsc[idx], self.sds[s], reads=[self.wsc_res[idx]], writes=[self.sres[s]])
        else:
            kb.dma("sp", self.slots[s][:].bitcast(F32), self.ada[idx], self.sds[s], reads=[], writes=[self.sres[s]])

    def get(self, kind, idx, hold=False):
        if self.kb.dry:
            self.plan.append((kind, idx))
            return self.slots[0], self.sres[0]
        i = self.pos
        self.pos += 1
        assert self.plan[i] == (kind, idx), (i, self.plan[i], kind, idx)
        if not hold:
            self.released = i
        lim = min(i + self.depth + 1, self.released + self.nslots, len(self.plan))
        while self.issued < lim:
            self._issue(self.issued)
            self.issued += 1
        return self.slots[i % self.nslots], self.sres[i % self.nslots]


class Rot:
    def __init__(self, items):
        self.items = items
        self.i = 0

    def next(self):
        it = self.items[self.i % len(self.items)]
        self.i += 1
        return it


def build_program(stage=99, dbg=False, skip=()):
    kb = KB()
    nc = kb.nc

    def din(name, shape, dt=F32):
        return nc.dram_tensor(name, list(shape), dt, kind="ExternalInput").ap()

    def dout(name, shape, dt=F32):
        return nc.dram_tensor(name, list(shape), dt, kind="ExternalOutput").ap()

    xin = din("xin", [NTOK, D])
    cin = din("cin", [1 + NSAMP, D])
    sst = din("sst", [NSAMP, NH, HD, DST])
    scv = din("scv", [NSAMP * 3, CONVD])
    w_all = din("w_all", [NSLAB, 128, 4096])
    w_ada = din("w_ada", [48, 128, 2048])
    pfm_d = din("pfm", [128, PF_N])
    p32_d = din("p32", [32, 2])
    cst_d = din("cst", [128, C_N])
    lngb_d = din("lngb", [2, DGM])
    wst_d = din("wst", [2, 128, NG * 128])
    bs_d = din("bs", [2, NG * 128])
    wdt_d = din("wdt", [128, KC * NH])
    seqm_d = din("seqm", [1, NSAMP * 128])

    y_out = dout("y_out", [NTOK, D])
    gmv_p = dout("gmv_p", [128, DGM])
    gmv_s = dout("gmv_s", [128, DGM])
    ssm_p = dout("ssm_p", [NH * HD, DST])
    cv_p = dout("cv_p", [3, CONVD])
    ssm_s = dout("ssm_s", [NSAMP, NH * HD, DST])
    cv_s = dout("cv_s", [NSAMP, 3, CONVD])
    wsc_t = nc.dram_tensor("wsc", [NSLAB, 128, 4096], BF16, kind="Internal").ap()

    cst = kb.sb("cst", [128, C_N], F32)
    pfm = kb.sb("pfmt", [128, PF_N], F32)
    p32 = kb.sb("p32t", [32, 2], F32)
    modT = kb.sb("modT", [128, 96, 17], F32)
    modA = kb.sb("modA", [128, 4, 8, 17], F32)
    x_fm = kb.sb("x_fm", [128, KC, 512], F32)
    h_fm = kb.sb("h_fm", [128, KC, 512], BF16)
    ones_bf = kb.sb("ones_bf", [128, 128], BF16)
    ones_g = kb.sb("ones_g", [128, 128], BF16)
    wstb = kb.sb("wstb", [128, NG * 128], BF16)
    Rt = kb.sb("Rt", [128, 16, 128], F32)
    stsm = kb.sb("stsm", [128, 3072], F32)
    smod = stsm[:, :].rearrange("p (a k t) -> p a k t", a=3, k=KC)
    ST = stsm[:, 0:2048]
    STb = stsm[:, 2048:3072].bitcast(BF16)
    wdt_bf = kb.sb("wdt_bf", [128, KC, NH], BF16)
    identb = kb.sb("identb", [128, 128], BF16)
    negib = kb.sb("negib", [128, 128], BF16)
    killb = kb.sb("killb", [128, 4, 128], BF16)
    halo = kb.sb("halo", [128, 32, 3], BF16)
    cvst = kb.sb("cvst", [128, 32, 3], F32)
    dtda_fm = kb.sb("dtda_fm", [32, 2, 256], F32)
    acol = kb.sb("acol", [32, 1], F32)
    ARENA_B = 84 * 1024
    arena = kb.sb("arena", [128, ARENA_B // 4], F32)
    banks = [nc.alloc_psum_tensor("bank%d" % i, [128, 512], F32).ap() for i in range(8)]
    bres = [kb.res("bank%d" % i) for i in range(8)]

    R_cst, R_pfm, R_p32 = kb.res("cst"), kb.res("pfm"), kb.res("p32")
    R_modT, R_modA = kb.res("modT"), kb.res("modA")
    R_x = [kb.res("x_fm%d" % k) for k in range(KC)]
    R_h = [kb.res("h_fm%d" % k) for k in range(KC)]
    R_ones = kb.res("ones")
    R_wstb, R_Rt, R_smod = kb.res("wstb"), kb.res("Rt"), kb.res("smod")
    wsc_res = [kb.res("wsc%d" % i) for i in range(NSLAB)]
    ds_cst = kb.dsem("cst")
    ds_out = kb.dsem("out")

    ws = WStream(kb, 6, 4, [wsc_t[i] for i in range(NSLAB)], wsc_res, [w_ada[i] for i in range(48)])

    arena_live = []
    dbg_names = []

    def dump(name, ap, reads, shape, dt=F32):
        if not dbg or kb.dry or name in dbg_names:
            return
        dbg_names.append(name)
        t = nc.dram_tensor("dbg_" + name, list(shape), dt, kind="ExternalOutput").ap()
        kb.dma("pool", t, ap, kb.dsem("dbg_" + name), reads=reads, is_out=True)

    def carve(name, off, shape, dtype):
        n = int(np.prod(shape[1:]))
        nbytes = n * (4 if dtype == F32 else 2)
        assert off % 4 == 0 and off + nbytes <= ARENA_B, (name, off, nbytes)
        v = arena[:, off // 4:(off + nbytes) // 4]
        if dtype != F32:
            v = v.bitcast(dtype)
        if len(shape) == 3:
            v = v.rearrange("p (a b) -> p a b", b=shape[2])
        elif len(shape) == 4:
            v = v.rearrange("p (a b c) -> p a b c", b=shape[2], c=shape[3])
        return v

    def new_phase(names):
        nonlocal arena_live
        fresh = [Res(n) for n in names]
        kb.fence_all(arena_live, fresh)
        arena_live = fresh
        return fresh

    def emit_cast(groups, gate):
        if kb.dry:
            return
        for gi in groups:
            (s0, s1) = SLAB_GROUPS[gi]
            ds = kb.dsem("cast%d" % gi)
            for s in range(s0, s1, 2):
                kb.dma("pool", wsc_t[s:s + 2].rearrange("s p (a e) -> (s p) a e", e=2048),
                       w_all[s:s + 2].rearrange("s p (a e) -> (s p) a e", e=2048), ds,
                       reads=gate, writes=[wsc_res[s], wsc_res[s + 1]])
            for s in range(s0, s1):
                wsc_res[s].w = (ds, ds.total)

    def emit():
        mmb = Rot([0, 1, 2, 3])
        auxb = Rot([4, 5, 6, 7])
        evq = Rot(["act", "dve"])

        if not kb.dry:
            emit_cast([0], [])
            for (dst, src, r) in ((cst, cst_d, R_cst), (pfm, pfm_d, R_pfm), (p32, p32_d, R_p32)):
                kb.dma("sp", dst[:], src[:], ds_cst, writes=[r])
            for r in (R_cst, R_pfm, R_p32):
                r.w = (ds_cst, ds_cst.total)

        ident = cst[:, C_ID:C_ID + 128]

        kb.op("dve", lambda: nc.vector.memset(ones_bf[:], 1.0 / 1024.0), writes=[R_ones])
        kb.op("dve", lambda: nc.vector.memset(ones_g[:], 1.0 / 256.0), writes=[R_ones])

        (R_ct,) = new_phase(["c_tm"])
        R_cT = kb.res("cT")
        c_tm = carve("c_tm", 0, [128, D], F32)
        cT = kb.sb("cT", [128, KC, 17], F32)
        ds_c = kb.dsem("cin")
        kb.dma("sp", c_tm[0:17, :], cin[:], ds_c, writes=[R_ct])
        kb.op("act", lambda: nc.scalar.activation(out=c_tm[0:17, :], in_=c_tm[0:17, :], func=AF.Silu),
              reads=[R_ct], writes=[R_ct])
        bk = auxb.next()
        for kc in range(KC):
            kb.op("pe", lambda kc=kc: nc.tensor.transpose(out=banks[bk][:, kc * 17:(kc + 1) * 17],
                                                          in_=c_tm[0:17, kc * 128:(kc + 1) * 128],
                                                          identity=ident[0:17, 0:17]),
                  reads=[R_ct, R_cst], writes=[bres[bk]])
        kb.op("dve", lambda: nc.vector.tensor_copy(out=cT[:].rearrange("p a b -> p (a b)"), in_=banks[bk][:, 0:KC * 17]),
              reads=[bres[bk]], writes=[R_cT])

        def ada_layer(l):
            (R_msm,) = new_phase(["mod_sm"])
            mod_sm = carve("mod_sm", 8192, [128, 6 * D], F32)
            for sl in range(24):
                slot, sr = ws.get("ada", l * 24 + sl)
                sv = slot[:].bitcast(F32).rearrange("p (k c) -> p k c", k=KC)
                if sl % 2 == 0:
                    bk = auxb.next()
                for kc in range(KC):
                    kb.op("pe", lambda sv=sv, kc=kc, bk=bk, sl=sl: nc.tensor.matmul(
                        banks[bk][0:17, (sl % 2) * 256:(sl % 2 + 1) * 256], lhsT=cT[:, kc, :], rhs=sv[:, kc, :],
                        start=(kc == 0), stop=(kc == KC - 1)), reads=[sr, R_cT], writes=[bres[bk]])
                if sl % 2 == 1:
                    e = evq.next()
                    dst = mod_sm[0:17, (sl - 1) * 256:(sl + 1) * 256]
                    if e == "act":
                        kb.op("act", lambda dst=dst, bk=bk: nc.scalar.copy(out=dst, in_=banks[bk][0:17, :]),
                              reads=[bres[bk]], writes=[R_msm])
                    else:
                        kb.op("dve", lambda dst=dst, bk=bk: nc.vector.tensor_copy(out=dst, in_=banks[bk][0:17, :]),
                              reads=[bres[bk]], writes=[R_msm])
            for half in range(2):
                bk = auxb.next()
                for mi in range(24):
                    m = half * 24 + mi
                    kb.op("pe", lambda m=m, mi=mi, bk=bk: nc.tensor.transpose(
                        out=banks[bk][:, mi * 17:(mi + 1) * 17], in_=mod_sm[0:17, m * 128:(m + 1) * 128],
                        identity=ident[0:17, 0:17]), reads=[R_msm, R_cst], writes=[bres[bk]])
                m0 = l * 48 + half * 24
                kb.op("dve", lambda m0=m0, bk=bk: nc.vector.tensor_tensor(
                    out=modT[:, m0:m0 + 24, :],
                    in0=banks[bk][:, 0:24 * 17].rearrange("p (m s) -> p m s", s=17),
                    in1=pfm[:, PF_ADAB + m0:PF_ADAB + m0 + 24].unsqueeze(2).broadcast_to([128, 24, 17]),
                    op=ALU.add), reads=[bres[bk], R_pfm], writes=[R_modT])
            for sub in range(2):
                sc0 = l * 48 + sub * 24 + 8
                gcol = (PF_N1G if sub == 0 else PF_N2G) + l * 8
                kb.op("dve", lambda sc0=sc0, gcol=gcol, sub=sub: nc.vector.scalar_tensor_tensor(
                    out=modA[:, l * 2 + sub, :, :], in0=modT[:, sc0:sc0 + 8, :], scalar=1.0, op0=ALU.add,
                    in1=pfm[:, gcol:gcol + 8].unsqueeze(2).broadcast_to([128, 8, 17]), op1=ALU.mult),
                    reads=[R_modT, R_pfm], writes=[R_modA])

            if l == 0:
                dump("modT", modT[:, 0:48, :], [R_modT], [128, 48, 17])

        def mod_cols(l, sub):
            b0 = l * 48 + sub * 24
            A = lambda kc: modA[:, l * 2 + sub, kc, 0:1]
            B = lambda kc: modT[:, b0 + kc, 0:1]
            G = lambda kc: modT[:, b0 + 16 + kc, 0:1]
            return A, B, G

        def build_smod(l, sub):
            b0 = l * 48 + sub * 24
            srcs = [modA[:, l * 2 + sub, :, 1:17], modT[:, b0:b0 + 8, 1:17], modT[:, b0 + 16:b0 + 24, 1:17]]
            for i, s in enumerate(srcs):
                kb.op("dve", lambda i=i, s=s: nc.vector.tensor_copy(
                    out=smod[:, i, :, :].rearrange("p k (b t) -> p k b t", t=LS),
                    in_=s.unsqueeze(3).broadcast_to([128, KC, NSAMP, LS])),
                    reads=[R_modT, R_modA], writes=[R_smod])

        def load_x(tiles, R_xt, xt):
            for ti, tg in enumerate(tiles):
                b = ti % 2
                kb.dma("sp", xt[:, b, :], xin[tg * 128:(tg + 1) * 128, :], kb.dsem("xt%d" % b), writes=[R_xt[b]])
                for hh in range(2):
                    bk = auxb.next()
                    for q in range(4):
                        kc = hh * 4 + q
                        kb.op("pe", lambda b=b, kc=kc, q=q, bk=bk: nc.tensor.transpose(
                            out=banks[bk][:, q * 128:(q + 1) * 128], in_=xt[:, b, kc * 128:(kc + 1) * 128],
                            identity=ident), reads=[R_xt[b], R_cst], writes=[bres[bk]])
                    e = evq.next()
                    dst = x_fm[:, hh * 4:hh * 4 + 4, ti * 128:(ti + 1) * 128]
                    src = banks[bk][:].rearrange("p (q t) -> p q t", t=128)
                    if e == "act":
                        kb.op("act", lambda dst=dst, src=src: nc.scalar.copy(out=dst, in_=src),
                              reads=[bres[bk]], writes=R_x[hh * 4:hh * 4 + 4])
                    else:
                        kb.op("dve", lambda dst=dst, src=src: nc.vector.tensor_copy(out=dst, in_=src),
                              reads=[bres[bk]], writes=R_x[hh * 4:hh * 4 + 4])

        def rms_rstd(T, srcs, src_res, sq_buf, R_sq, ones_t, eps_scale=1.0):
            n = len(srcs)
            for i in range(n):
                kb.op("act", lambda i=i: nc.scalar.activation(out=sq_buf[:, i, 0:T], in_=srcs[i], func=AF.Square),
                      reads=[src_res[i]], writes=[R_sq[i]])
            bk = auxb.next()
            for i in range(n):
                kb.op("pe", lambda i=i: nc.tensor.matmul(banks[bk][:, 0:T], lhsT=ones_t[:], rhs=sq_buf[:, i, 0:T],
                                                         start=(i == 0), stop=(i == n - 1)),
                      reads=[R_sq[i], R_ones], writes=[bres[bk]], lhs=[R_ones])
            kb.op("act", lambda: nc.scalar.activation(out=banks[bk][:, 0:T], in_=banks[bk][:, 0:T], func=AF.Ln,
                                                      bias=eps_t[:, 0:1], scale=1.0),
                  reads=[bres[bk], R_ones], writes=[bres[bk]])
            kb.op("act", lambda: nc.scalar.activation(out=banks[bk][:, 0:T], in_=banks[bk][:, 0:T], func=AF.Exp, scale=-0.5),
                  reads=[bres[bk]], writes=[bres[bk]])
            return bk

        eps_t = kb.sb("eps_t", [128, 1], F32)
        mhalf = kb.sb("mhalf", [128, 1], F32)
        kb.op("dve", lambda: nc.vector.memset(eps_t[:], EPS), writes=[R_ones])
        kb.op("dve", lambda: nc.vector.memset(mhalf[:], -0.5), writes=[R_ones])

        def norm_mod(T, l, sub, sample, sq_buf, R_sq, tmp, R_tmp):
            bk = rms_rstd(T, [x_fm[:, kc, 0:T] for kc in range(KC)], R_x, sq_buf, R_sq, ones_bf)
            if not sample:
                A, B, G = mod_cols(l, sub)
                for kc in range(KC):
                    t = tmp[:, kc % 2, 0:T]
                    kb.op("dve", lambda kc=kc, t=t: nc.vector.scalar_tensor_tensor(
                        out=t, in0=x_fm[:, kc, 0:T], scalar=A(kc), op0=ALU.mult, in1=banks[bk][:, 0:T], op1=ALU.mult),
                        reads=[R_x[kc], bres[bk], R_modA], writes=[R_tmp[kc % 2]])
                    kb.op("act", lambda kc=kc, t=t: nc.scalar.activation(
                        out=h_fm[:, kc, 0:T], in_=t, func=AF.Identity, bias=B(kc), scale=1.0),
                        reads=[R_tmp[kc % 2], R_modT], writes=[R_h[kc]])
            else:
                build_smod(l, sub)
                for kc in range(KC):
                    t = tmp[:, kc % 2, 0:T]
                    kb.op("dve", lambda kc=kc, t=t: nc.vector.tensor_tensor(
                        out=t, in0=x_fm[:, kc, 0:T], in1=banks[bk][:, 0:T], op=ALU.mult),
                        reads=[R_x[kc], bres[bk]], writes=[R_tmp[kc % 2]])
                    kb.op("dve", lambda kc=kc, t=t: nc.vector.tensor_tensor(
                        out=t, in0=t, in1=smod[:, 0, kc, :], op=ALU.mult),
                        reads=[R_tmp[kc % 2], R_smod], writes=[R_tmp[kc % 2]])
                    kb.op("dve", lambda kc=kc, t=t: nc.vector.tensor_tensor(
                        out=h_fm[:, kc, 0:T], in0=t, in1=smod[:, 1, kc, :], op=ALU.add),
                        reads=[R_tmp[kc % 2], R_smod], writes=[R_h[kc]])

        def resid_add(T, c, bk, l, sub, sample, tmp, R_tmp, c0=0):
            xs = x_fm[:, c, c0:c0 + T]
            if not sample:
                A, B, G = mod_cols(l, sub)
                kb.op("dve", lambda: nc.vector.scalar_tensor_tensor(
                    out=xs, in0=banks[bk][:, 0:T], scalar=G(c), op0=ALU.mult, in1=xs, op1=ALU.add),
                    reads=[bres[bk], R_modT, R_x[c]], writes=[R_x[c]])
            else:
                t = tmp[:, c % 2, 0:T]
                kb.op("dve", lambda: nc.vector.tensor_tensor(out=t, in0=banks[bk][:, 0:T], in1=smod[:, 2, c, :], op=ALU.mult),
                      reads=[bres[bk], R_smod], writes=[R_tmp[c % 2]])
                kb.op("dve", lambda: nc.vector.tensor_tensor(out=xs, in0=xs, in1=t, op=ALU.add),
                      reads=[R_tmp[c % 2], R_x[c]], writes=[R_x[c]])

        def gmlp_consts(v):
            (R_w32, R_bsb) = new_phase(["w32", "bsb"])
            w32 = carve("w32", 0, [128, NG * 128], F32)
            bsb = carve("bsb", 4096, [128, NG * 128], F32)
            if True:
                kb.dma("sp", w32[:], wst_d[v], kb.dsem("gc_w"), writes=[R_w32])
                kb.dma("sp", bsb[:], bs_d[v:v + 1, :].broadcast_to([128, NG * 128]), kb.dsem("gc_b"), writes=[R_bsb])
                mcol = C_LT if v == 0 else C_LTB
                kb.op("dve", lambda v=v, mcol=mcol: nc.vector.tensor_tensor(
                    out=w32[:].rearrange("p (g t) -> p g t", t=128), in0=w32[:].rearrange("p (g t) -> p g t", t=128),
                    in1=cst[:, mcol:mcol + 128].unsqueeze(1).broadcast_to([128, NG, 128]), op=ALU.mult),
                    reads=[R_w32, R_cst], writes=[R_w32])
                kb.op("act", lambda v=v: nc.scalar.copy(out=wstb[:, :], in_=w32[:]), reads=[R_w32], writes=[R_wstb])
                bks = [auxb.next(), auxb.next()]
                for hh in range(2):
                    kb.op("pe", lambda hh=hh, bks=bks: nc.tensor.matmul(
                        banks[bks[hh]][:, :], lhsT=cst[:, C_ONES:C_ONES + 128], rhs=w32[:, hh * 512:(hh + 1) * 512],
                        start=True, stop=True), reads=[R_w32, R_cst], writes=[bres[bks[hh]]])
                for j in range(16):
                    g = j // 2
                    bk = bks[g // 4]
                    kb.op("dve", lambda v=v, j=j, g=g, bk=bk: nc.vector.scalar_tensor_tensor(
                        out=Rt[:, j, :], in0=banks[bk][:, (g % 4) * 128:(g % 4 + 1) * 128],
                        scalar=pfm[:, PF_LNB + j:PF_LNB + j + 1], op0=ALU.mult,
                        in1=bsb[:, g * 128:(g + 1) * 128], op1=ALU.add),
                        reads=[bres[bk], R_pfm, R_bsb], writes=[R_Rt])

        def gmlp(tiles, l, sample):
            nt = len(tiles)
            T = nt * 128
            var = 1 if sample else 0
            names = ["z0", "z1", "n0", "n1", "n2", "n3", "u0", "u1", "us", "lng", "lnb", "ssb0", "ssb1", "st"]
            rr = new_phase(names)
            R_z = rr[0:2]; R_n = rr[2:6]; R_u = rr[6:8]; R_us = rr[8]; R_lng = rr[9]; R_lnb = rr[10]; R_ssb = rr[11:13]; R_st = rr[13]
            z_tm = carve("z_tm", 0, [128, 2, DGM], F32)
            n_tm = carve("n_tm", 16384, [128, 4, DGM], BF16)
            u_tmp = carve("u_tmp", 32768, [128, 2, 512], F32)
            us_fm = carve("us_fm", 36864, [128, 16, 512], BF16)
            lng_b = carve("lng_b", 53248, [128, DGM], F32)
            lnb_b = carve("lnb_b", 61440, [128, DGM], F32)
            ssb = carve("ssb", 69632, [128, 2, 128 * 2], F32)
            st = kb.sb("gm_stats", [128, 16], F32)
            bst = kb.sb("gm_bst", [128, 24], F32)
            R_us16 = [Res("us%d" % j) for j in range(16)]
            kb.fence_all([R_us], R_us16)
            vslots = [ws.get("sc", SL_GV + nb, hold=(nb > 0)) for nb in range(4)]
            for ti, tg in enumerate(tiles):
                zb = ti % 2
                need_v = sample or tg == 15
                if need_v and ti == nt - 1:
                    kb.dma("sp", lng_b[:], lngb_d[0:1, :].broadcast_to([128, DGM]), kb.dsem("lng"), writes=[R_lng])
                    kb.dma("sp", lnb_b[:], lngb_d[1:2, :].broadcast_to([128, DGM]), kb.dsem("lnb"), writes=[R_lnb])
                for nb in range(4):
                    slot, sr = vslots[nb]
                    sv = slot[:].rearrange("p (k c) -> p k c", c=512)
                    bk = mmb.next()
                    for kc in range(KC):
                        kb.op("pe", lambda kc=kc, sv=sv, bk=bk, ti=ti: nc.tensor.matmul(
                            banks[bk][:, :], lhsT=h_fm[:, kc, ti * 128:(ti + 1) * 128], rhs=sv[:, kc, :],
                            start=(kc == 0), stop=(kc == KC - 1)), reads=[sr, R_h[kc]], writes=[bres[bk]], lhs=[R_h[kc]])
                    kb.op("act", lambda nb=nb, bk=bk, zb=zb: nc.scalar.activation(
                        out=z_tm[:, zb, nb * 512:(nb + 1) * 512], in_=banks[bk][:, :], func=AF.Gelu_apprx_tanh),
                        reads=[bres[bk]], writes=[R_z[zb]])
                    kb.op("dve", lambda nb=nb, zb=zb: nc.vector.bn_stats(out=bst[:, nb * 6:(nb + 1) * 6],
                                                                         in_=z_tm[:, zb, nb * 512:(nb + 1) * 512]),
                          reads=[R_z[zb]], writes=[R_st])
                kb.op("dve", lambda: nc.vector.bn_aggr(out=st[:, 6:8], in_=bst[:, 0:24]), reads=[R_st], writes=[R_st])
                kb.op("dve", lambda: nc.vector.tensor_scalar(out=st[:, 8:9], in0=st[:, 7:8], scalar1=EPS, scalar2=None, op0=ALU.add),
                      reads=[R_st], writes=[R_st])
                kb.op("pool", lambda: nc.gpsimd.tensor_tensor(out=st[:, 10:11], in0=st[:, 8:9], in1=mhalf[:, 0:1], op=ALU.pow),
                      reads=[R_st, R_ones], writes=[R_st])
                kb.op("dve", lambda zb=zb, ti=ti: nc.vector.tensor_scalar(
                    out=n_tm[:, ti, :], in0=z_tm[:, zb, :], scalar1=st[:, 6:7], scalar2=st[:, 10:11],
                    op0=ALU.subtract, op1=ALU.mult), reads=[R_z[zb], R_st], writes=[R_n[ti]])
                if need_v:
                    kb.op("dve", lambda: nc.vector.scalar_tensor_tensor(out=st[:, 11:12], in0=st[:, 6:7], scalar=-1.0,
                                                                        op0=ALU.mult, in1=st[:, 10:11], op1=ALU.mult),
                          reads=[R_st], writes=[R_st])
                if ti == 0:
                    dump("n_tm", n_tm[:, 0, :], [R_n[0]], [128, DGM], BF16)
                    dump("z_tm", z_tm[:, 0, :], [R_z[0]], [128, DGM])
                if need_v:
                    kb.op("act", lambda zb=zb: nc.scalar.activation(
                        out=z_tm[:, zb, :], in_=z_tm[:, zb, :], func=AF.Identity, bias=st[:, 11:12], scale=st[:, 10:11]),
                        reads=[R_z[zb], R_st], writes=[R_z[zb]])
                    kb.op("dve", lambda zb=zb: nc.vector.tensor_tensor(out=z_tm[:, zb, :], in0=z_tm[:, zb, :], in1=lng_b[:],
                                                                       op=ALU.mult), reads=[R_z[zb], R_lng], writes=[R_z[zb]])
                    kb.op("dve", lambda zb=zb: nc.vector.tensor_tensor(out=z_tm[:, zb, :], in0=z_tm[:, zb, :], in1=lnb_b[:],
                                                                       op=ALU.add), reads=[R_z[zb], R_lnb], writes=[R_z[zb]])
                    kb.dma("pool", (gmv_s if sample else gmv_p)[:, :], z_tm[:, zb, :], kb.dsem("st_z%d" % zb), reads=[R_z[zb]], is_out=True)
            for su in range(4):
                slot, sr = ws.get("sc", SL_GU + su)
                sv = slot[:].rearrange("p (m k c) -> p m k c", m=4, k=KC)
                for mi in range(4):
                    j = su * 4 + mi
                    g = j // 2
                    ub = j % 2
                    bk = mmb.next()
                    for kc in range(KC):
                        kb.op("pe", lambda kc=kc, sv=sv, mi=mi, bk=bk: nc.tensor.matmul(
                            banks[bk][:, 0:T], lhsT=sv[:, mi, kc, :], rhs=h_fm[:, kc, 0:T],
                            start=(kc == 0), stop=(kc == KC - 1)), reads=[sr, R_h[kc]], writes=[bres[bk]], lhs=[sr])
                    kb.op("act", lambda bk=bk, ub=ub: nc.scalar.activation(out=u_tmp[:, ub, 0:T], in_=banks[bk][:, 0:T],
                                                                           func=AF.Gelu_apprx_tanh),
                          reads=[bres[bk]], writes=[R_u[ub]])
                    bk2 = auxb.next()
                    for ti in range(nt):
                        kb.op("pe", lambda ti=ti, j=j, g=g, bk2=bk2: nc.tensor.matmul(
                            banks[bk2][:, ti * 128:(ti + 1) * 128], lhsT=n_tm[:, ti, j * 128:(j + 1) * 128],
                            rhs=wstb[:, g * 128:(g + 1) * 128], start=True, stop=True),
                            reads=[R_n[ti], R_wstb], writes=[bres[bk2]], lhs=[R_n[ti]])
                    sb_ = ssb[:, ub, :].rearrange("p (a b) -> p a b", b=128) if False else None
                    s_t = kb.sb("gm_s%d" % ub, [128, 512], F32)
                    kb.op("dve", lambda j=j, bk2=bk2, s_t=s_t: nc.vector.scalar_tensor_tensor(
                        out=s_t[:, 0:T].rearrange("p (a b) -> p a b", b=128),
                        in0=banks[bk2][:, 0:T].rearrange("p (a b) -> p a b", b=128),
                        scalar=pfm[:, PF_LNG + j:PF_LNG + j + 1], op0=ALU.mult,
                        in1=Rt[:, j, :].unsqueeze(1).broadcast_to([128, nt, 128]), op1=ALU.add),
                        reads=[bres[bk2], R_pfm, R_Rt], writes=[R_ssb[ub]])
                    kb.op("pool", lambda j=j, ub=ub, s_t=s_t: nc.gpsimd.tensor_tensor(
                        out=us_fm[:, j, 0:T], in0=u_tmp[:, ub, 0:T], in1=s_t[:, 0:T], op=ALU.mult),
                        reads=[R_u[ub], R_ssb[ub]], writes=[R_us16[j]])
            tmpR = [Res("gtmp0"), Res("gtmp1")]
            kb.fence_all(R_z, tmpR)
            tmp = z_tm[:, :, 0:512]
            for so in range(4):
                slot, sr = ws.get("sc", SL_GO + so)
                sv = slot[:].rearrange("p (m k c) -> p m k c", m=2, k=16)
                for mi in range(2):
                    c = so * 2 + mi
                    bk = mmb.next()
                    for kc in range(16):
                        kb.op("pe", lambda kc=kc, sv=sv, mi=mi, bk=bk: nc.tensor.matmul(
                            banks[bk][:, 0:T], lhsT=sv[:, mi, kc, :], rhs=us_fm[:, kc, 0:T],
                            start=(kc == 0), stop=(kc == 15)), reads=[sr, R_us16[kc]], writes=[bres[bk]], lhs=[sr])
                    resid_add(T, c, bk, l, 0, sample, tmp, tmpR)
            arena_live.extend(R_us16 + tmpR)

        def mlp(T, l, sample):
            rr = new_phase(["sqa", "sqb", "nt0", "nt1"] + ["nsq%d" % k for k in range(KC)] + ["hid%d" % j for j in range(32)])
            R_sqt = rr[0:2]; R_tmp = rr[2:4]; R_sq = rr[4:12]; R_hid = rr[12:44]
            hid = carve("hid", 0, [128, 32, 512], BF16)
            sqt = carve("sqt", 32768, [128, 2, 512], F32)
            tmp = carve("ntmp", 36864, [128, 2, 512], F32)
            sq_buf = carve("nsq", 40960, [128, KC, 512], BF16)
            norm_mod(T, l, 1, sample, sq_buf, R_sq, tmp, R_tmp)
            s1 = SL_M1_0 if l == 0 else SL_M1_1
            s2 = SL_M2_0 if l == 0 else SL_M2_1
            for s in range(8):
                slot, sr = ws.get("sc", s1 + s)
                sv = slot[:].rearrange("p (m k c) -> p m k c", m=4, k=KC)
                for mi in range(4):
                    j = s * 4 + mi
                    bk = mmb.next()
                    for kc in range(KC):
                        kb.op("pe", lambda kc=kc, sv=sv, mi=mi, bk=bk: nc.tensor.matmul(
                            banks[bk][:, 0:T], lhsT=sv[:, mi, kc, :], rhs=h_fm[:, kc, 0:T],
                            start=(kc == 0), stop=(kc == KC - 1)), reads=[sr, R_h[kc]], writes=[bres[bk]], lhs=[sr])
                    qb = j % 2
                    kb.op("act", lambda bk=bk, qb=qb: nc.scalar.activation(out=sqt[:, qb, 0:T], in_=banks[bk][:, 0:T],
                                                                           func=AF.Square),
                          reads=[bres[bk]], writes=[R_sqt[qb]])
                    kb.op("dve", lambda bk=bk, qb=qb, j=j: nc.vector.scalar_tensor_tensor(
                        out=hid[:, j, 0:T], in0=banks[bk][:, 0:T], scalar=0.0, op0=ALU.is_gt, in1=sqt[:, qb, 0:T],
                        op1=ALU.mult), reads=[bres[bk], R_sqt[qb]], writes=[R_hid[j]])
            if l == 0 and T == 512:
                dump("hid", hid[:, :, :], R_hid, [128, 32, 512], BF16)
                dump("h2", h_fm[:, :, :], R_h, [128, KC, 512], BF16)
            for c in range(8):
                slot, sr = ws.get("sc", s2 + c)
                sv = slot[:].rearrange("p (k c) -> p k c", c=128)
                bk = mmb.next()
                for kc in range(32):
                    kb.op("pe", lambda kc=kc, sv=sv, bk=bk: nc.tensor.matmul(
                        banks[bk][:, 0:T], lhsT=sv[:, kc, :], rhs=hid[:, kc, 0:T], start=(kc == 0), stop=(kc == 31)),
                        reads=[sr, R_hid[kc]], writes=[bres[bk]], lhs=[sr])
                resid_add(T, c, bk, l, 1, sample, tmp, R_tmp)


        R_ST, R_STb, R_halo, R_cvst, R_mc = kb.res("ST"), kb.res("STb"), kb.res("halo"), kb.res("cvst"), kb.res("mconst")

        def mamba_consts():
            (R_w,) = new_phase(["wdt32"])
            w32 = carve("wdt32", 0, [128, KC * NH], F32)
            kb.dma("sp", w32[:], wdt_d[:, :], kb.dsem("wdt"), writes=[R_w])
            kb.op("dve", lambda: nc.vector.tensor_copy(out=wdt_bf[:].rearrange("p k h -> p (k h)"), in_=w32[:]),
                  reads=[R_w], writes=[R_mc])
            kb.op("dve", lambda: nc.vector.tensor_copy(out=identb[:], in_=ident), reads=[R_cst], writes=[R_mc])
            kb.op("dve", lambda: nc.vector.tensor_copy(out=negib[:], in_=cst[:, C_NEGI:C_NEGI + 128]), reads=[R_cst], writes=[R_mc])
            kb.op("act", lambda: nc.scalar.activation(out=acol[:], in_=p32[:, 1:2], func=AF.Exp), reads=[R_p32], writes=[R_mc])
            kb.op("dve", lambda: nc.vector.tensor_scalar(out=acol[:], in0=acol[:], scalar1=-1.0, scalar2=None, op0=ALU.mult),
                  reads=[R_mc], writes=[R_mc])
            kb.op("dve", lambda: nc.vector.memset(ST, 0.0), writes=[R_ST])
            kb.op("dve", lambda: nc.vector.memset(STb, 0.0), writes=[R_STb])
            kb.op("dve", lambda: nc.vector.memset(halo[:], 0.0), writes=[R_halo])

        def mamba_masks(sample):
            kc_ = C_KILLB if sample else C_KILL
            kb.op("dve", lambda: nc.vector.tensor_copy(
                out=killb[:], in_=cst[:, kc_:kc_ + 128].unsqueeze(1).broadcast_to([128, 4, 128])),
                reads=[R_cst], writes=[R_mc])

        def mamba(tiles, c0, sample, last_prompt):
            nt = len(tiles)
            T = nt * 128
            TP = 256 if not sample else 128
            l = 1
            LTm = cst[:, (C_LTB if sample else C_LT):(C_LTB if sample else C_LT) + 128]
            Um = cst[:, (C_UB if sample else C_U):(C_UB if sample else C_U) + 128]
            ONm = cst[:, (C_SAME if sample else C_ONES):(C_SAME if sample else C_ONES) + 128]
            names = (["zs%d" % i for i in range(16)] + ["xc%d" % i for i in range(16)] + ["bc%d" % i for i in range(16)]
                     + ["yg%d" % i for i in range(16)]
                     + ["xdt", "xdtd", "btm", "rda", "E", "M", "ytm", "t1", "xpre0", "xpre1", "dg0", "dg1", "small", "yf", "yz", "ysq",
                        "ntmp0", "ntmp1", "dtda"])
            rr = new_phase(names)
            R_zs = rr[0:16]; R_xc = rr[16:32]; R_bc = rr[32:48]; R_yg = rr[48:64]
            (R_xdt, R_xdtd, R_btm, _r1, _r2, _r3, _r4, R_t1, R_xp0, R_xp1, R_dg0, R_dg1, R_small, _r5, _r6, _r7,
             R_nt0, R_nt1, R_dtda) = rr[64:]
            R_xp = [R_xp0, R_xp1]; R_dgp = [R_dg0, R_dg1]
            R_dgv = [Res("dgv0"), Res("dgv1")]
            for r_ in R_dgv:
                r_.r = dict(rr[0].r)
            arena_live.extend(R_dgv)
            o = 0
            def cv(name, shape, dt):
                nonlocal o
                nb = int(np.prod(shape[1:])) * (4 if dt == F32 else 2)
                nb = (nb + 31) // 32 * 32
                v = carve(name, o, shape, dt)
                o += nb
                return v
            zs = cv("zs", [128, 16, TP], BF16)
            xc = cv("xc", [128, 16, TP], F32)
            BC = cv("BC", [128, 16, TP], BF16)
            yg = cv("yg", [128, 16, TP], BF16)
            xdt = cv("xdt", [128, 2048], BF16)
            xdtd = cv("xdtd", [128, 2048], BF16)
            B_tm = cv("B_tm", [128, 1024], BF16)
            NB = 1 if sample else 2
            rda = cv("rda", [128, NB, 4, 128], F32)
            E = cv("E", [128, NB, 512], F32)
            Mh = cv("Mh", [128, 2 * NB, 512], BF16)
            y_tm = cv("y_tm", [128, NB, 512], F32)
            t1 = cv("t1", [128, 512], F32) if not sample else None
            xpre = cv("xpre", [128, 2, TP + 8], BF16)
            dg = cv("dg", [128, 2, 4, 128], BF16)
            small = cv("small", [128, 8, 32], F32)
            yf = cv("yf", [128, NB, 4, 128], F32)
            yz = cv("yz", [128, NB, 4, 128], F32)
            ysq = cv("ysq", [128, NB, 4, 128], BF16)
            ntmp = cv("ntmp", [128, 2, TP], F32) if sample else None
            R_rda = [Res("rda%d" % i) for i in range(NB)]
            R_E = [Res("E%d" % i) for i in range(NB)]
            R_M = [Res("M%d" % i) for i in range(2 * NB)]
            R_ytm = [Res("ytm%d" % i) for i in range(NB)]
            R_yf = [Res("yf%d" % i) for i in range(NB)]
            R_yz = [Res("yz%d" % i) for i in range(NB)]
            R_ysq = [Res("ysq%d" % i) for i in range(NB)]
            extra = R_rda + R_E + R_M + R_ytm + R_yf + R_yz + R_ysq
            for r_ in extra:
                r_.r = dict(rr[0].r)
            arena_live.extend(extra)
            if sample:
                S0bf = cv("S0bf", [128, 2048], BF16)
                S0T = cv("S0T", [128, 2048], BF16)
                t1_all = cv("t1_all", [128, 2048], F32)
                seqmask = cv("seqmask", [128, NSAMP, 128], BF16)
                Cm = cv("Cm", [128, NG, 128], BF16)
                Bm = cv("Bm", [128, 1024], BF16)
                cvs6 = cv("cvs6", [128, 32, 48], F32)
                cd_fm = cv("cd_fm", [128, 256], F32)
                halo_s = cv("halo_s", [128, 32, 48], BF16)
                xpre_s = cv("xpre_s", [128, 2, NSAMP, 11], BF16)
                cvs6_off = o - 0
                (R_S0bf, R_S0T, R_t1all, R_seqm, R_Cm, R_Bm, R_cvs6, R_cdfm, R_halos, R_scv, R_S0a, R_S0b) = [
                    Res(n) for n in ("S0bf", "S0T", "t1all", "seqm", "Cm", "Bm", "cvs6", "cdfm", "halos", "scv", "S0a", "S0b")]
                fresh = [R_S0bf, R_S0T, R_t1all, R_seqm, R_Cm, R_Bm, R_cvs6, R_cdfm, R_halos, R_scv]
                for r_ in fresh:
                    r_.r = dict(rr[0].r)
                arena_live.extend(fresh + [R_S0a, R_S0b])
                kb.fence_all([R_Rt], [R_S0a])
                kb.fence_all(R_x, [R_S0b])
                S0buf = [Rt[:, :, :].rearrange("p (k two) n -> p k two n", two=2),
                         x_fm[:, :, 128:384].rearrange("p k (two n) -> p k two n", two=2)]
                R_S0 = [R_S0a, R_S0b]
                stg = carve("scvstg", 0, [128, CONVD], F32)
                kb.dma("sp", stg[0:48, :], scv[:, :], kb.dsem("scv"), writes=[R_scv])
                for r in range(4):
                    bk = auxb.next()
                    for i in range(8):
                        j = r * 8 + i
                        kb.op("pe", lambda j=j, i=i, bk=bk: nc.tensor.transpose(out=banks[bk][:, i * 48:(i + 1) * 48],
                                                                                in_=stg[0:48, j * 128:(j + 1) * 128],
                                                                                identity=ident[0:48, 0:48]),
                              reads=[R_scv, R_cst], writes=[bres[bk]])
                    kb.op("dve", lambda r=r, bk=bk: nc.vector.tensor_copy(
                        out=halo_s[:, r * 8:(r + 1) * 8, :].rearrange("p a b -> p (a b)"), in_=banks[bk][:, 0:384]),
                        reads=[bres[bk]], writes=[R_halos])
                kb.fence_all([R_scv], R_zs + R_xc + R_bc)
            dt_tm = small[:, 0, :]; da_tm = small[:, 1, :]; acum_sb = small[:, 2, :]; ea = small[:, 3, :]
            dte = small[:, 4, :]; w2 = small[:, 5, :]; cdv = small[:, 6, :]; dtmp = small[:, 7, :]
            cs = slice(c0, c0 + T)

            for s_ in range(4):
                slot, sr = ws.get("sc", SL_SI + s_)
                sv = slot[:].rearrange("p (m k c) -> p m k c", m=4, k=KC)
                for mi in range(4):
                    m = s_ * 4 + mi
                    bk = mmb.next()
                    for kc in range(KC):
                        kb.op("pe", lambda kc=kc, sv=sv, mi=mi, bk=bk: nc.tensor.matmul(
                            banks[bk][:, 0:T], lhsT=sv[:, mi, kc, :], rhs=h_fm[:, kc, cs],
                            start=(kc == 0), stop=(kc == KC - 1)), reads=[sr, R_h[kc]], writes=[bres[bk]], lhs=[sr])
                    kb.op("act", lambda bk=bk, m=m: nc.scalar.activation(out=zs[:, m, 0:T], in_=banks[bk][:, 0:T], func=AF.Silu),
                          reads=[bres[bk]], writes=[R_zs[m]])
            pending = None
            for s_ in range(8):
                slot, sr = ws.get("sc", SL_SI + 4 + s_)
                sv = slot[:].rearrange("p (m k c) -> p m k c", m=4, k=KC)
                for mi in range(4):
                    j = s_ * 4 + mi
                    jb = j % 2
                    bk = mmb.next()
                    for kc in range(KC):
                        kb.op("pe", lambda kc=kc, sv=sv, mi=mi, bk=bk: nc.tensor.matmul(
                            banks[bk][:, 0:T], lhsT=sv[:, mi, kc, :], rhs=h_fm[:, kc, cs],
                            start=(kc == 0), stop=(kc == KC - 1)), reads=[sr, R_h[kc]], writes=[bres[bk]], lhs=[sr])
                    for k in range(4):
                        if k < 2:
                            kb.op("pool", lambda k=k, j=j, jb=jb: nc.gpsimd.tensor_scalar(
                                out=dg[:, jb, k, :], in0=identb[:], scalar1=pfm[:, PF_CW + k * 32 + j:PF_CW + k * 32 + j + 1],
                                scalar2=1.0, op0=ALU.mult, op1=ALU.mult), reads=[R_mc, R_pfm], writes=[R_dgp[jb]])
                        else:
                            kb.op("dve", lambda k=k, j=j, jb=jb: nc.vector.tensor_scalar(
                                out=dg[:, jb, k, :], in0=identb[:], scalar1=pfm[:, PF_CW + k * 32 + j:PF_CW + k * 32 + j + 1],
                                scalar2=None, op0=ALU.mult), reads=[R_mc, R_pfm], writes=[R_dgv[jb]])
                    if sample:
                        kb.op("dve", lambda j=j, jb=jb: nc.vector.tensor_copy(
                            out=xpre_s[:, jb, :, 0:3], in_=halo_s[:, j, :].rearrange("p (b k) -> p b k", k=3)),
                            reads=[R_halos], writes=[R_xp[jb]])
                        kb.op("dve", lambda bk=bk, jb=jb: nc.vector.tensor_copy(
                            out=xpre_s[:, jb, :, 3:11], in_=banks[bk][:, 0:128].rearrange("p (b t) -> p b t", t=LS)),
                            reads=[bres[bk]], writes=[R_xp[jb]])
                        kb.op("dve", lambda bk=bk, j=j: nc.vector.tensor_copy(
                            out=cvs6[:, j, :].rearrange("p (b k) -> p b k", k=3),
                            in_=banks[bk][:, 0:128].rearrange("p (b t) -> p b t", t=LS)[:, :, 5:8]),
                            reads=[bres[bk]], writes=[R_cvs6])
                    else:
                        kb.op("dve", lambda j=j, jb=jb: nc.vector.tensor_copy(out=xpre[:, jb, 0:3], in_=halo[:, j, :]),
                              reads=[R_halo], writes=[R_xp[jb]])
                        kb.op("dve", lambda bk=bk, jb=jb: nc.vector.tensor_copy(out=xpre[:, jb, 3:3 + T], in_=banks[bk][:, 0:T]),
                              reads=[bres[bk]], writes=[R_xp[jb]])
                        kb.op("dve", lambda j=j, jb=jb: nc.vector.tensor_copy(out=halo[:, j, :], in_=xpre[:, jb, T:T + 3]),
                              reads=[R_xp[jb]], writes=[R_halo])
                    if last_prompt:
                        kb.op("dve", lambda j=j, bk=bk: nc.vector.tensor_copy(out=cvst[:, j, :], in_=banks[bk][:, T - 3:T]),
                              reads=[bres[bk]], writes=[R_cvst])
                    def conv_emit(j=j, jb=jb):
                        b2 = auxb.next()
                        for k in range(4):
                            kb.op("pe", lambda k=k, jb=jb, b2=b2: nc.tensor.matmul(
                                banks[b2][:, 0:T], lhsT=dg[:, jb, k, :],
                                rhs=(xpre_s[:, jb, :, k:k + LS] if sample else xpre[:, jb, k:k + T]), start=(k == 0), stop=(k == 3)),
                                reads=[R_dgp[jb], R_dgv[jb], R_xp[jb]], writes=[bres[b2]], lhs=[R_dgp[jb], R_dgv[jb]])
                        if j < 16:
                            kb.op("act", lambda j=j, b2=b2: nc.scalar.activation(
                                out=xc[:, j, 0:T], in_=banks[b2][:, 0:T], func=AF.Silu, bias=pfm[:, PF_CB + j:PF_CB + j + 1], scale=1.0),
                                reads=[bres[b2], R_pfm], writes=[R_xc[j]])
                        else:
                            kb.op("act", lambda j=j, b2=b2: nc.scalar.activation(
                                out=BC[:, j - 16, 0:T], in_=banks[b2][:, 0:T], func=AF.Silu, bias=pfm[:, PF_CB + j:PF_CB + j + 1], scale=1.0),
                                reads=[bres[b2], R_pfm], writes=[R_bc[j - 16]])
                    if pending is not None:
                        pending()
                    pending = conv_emit
            pending()
            bk = mmb.next()
            for kc in range(KC):
                kb.op("pe", lambda kc=kc, bk=bk: nc.tensor.matmul(
                    banks[bk][0:32, 0:T], lhsT=wdt_bf[:, kc, :], rhs=h_fm[:, kc, cs], start=(kc == 0), stop=(kc == KC - 1)),
                    reads=[R_mc, R_h[kc]], writes=[bres[bk]])
            kb.op("act", lambda bk=bk: nc.scalar.activation(out=dtda_fm[:, 0, 0:T], in_=banks[bk][0:32, 0:T], func=AF.Softplus,
                                                            bias=p32[:, 0:1], scale=1.0),
                  reads=[bres[bk], R_p32], writes=[R_dtda])
            kb.op("dve", lambda: nc.vector.tensor_scalar(out=dtda_fm[:, 1, 0:T], in0=dtda_fm[:, 0, 0:T], scalar1=acol[:, 0:1],
                                                         scalar2=None, op0=ALU.mult), reads=[R_dtda, R_mc], writes=[R_dtda])

            early_ap = []
            for ci in range(nt):
                cc = slice(ci * 128, (ci + 1) * 128)
                for i in range(2):
                    kb.op("pe", lambda i=i: nc.tensor.transpose(out=banks[6][:, i * 32:(i + 1) * 32], in_=dtda_fm[:, i, cc],
                                                                identity=ident[0:32, 0:32]),
                          reads=[R_dtda, R_cst], writes=[bres[6]])
                kb.op("dve", lambda: nc.vector.tensor_copy(out=small[:, 0:2, :].rearrange("p a h -> p (a h)"), in_=banks[6][:, 0:64]),
                      reads=[bres[6]], writes=[R_small])
                for g_ in range(2):
                    kb.op("pool", lambda g_=g_: nc.gpsimd.tensor_tensor(
                        out=rda[:, g_ % NB, :, :], in0=LTm.unsqueeze(1).broadcast_to([128, 4, 128]),
                        in1=da_tm[:, g_ * 4:(g_ + 1) * 4].unsqueeze(2).broadcast_to([128, 4, 128]), op=ALU.mult),
                        reads=[R_cst, R_small], writes=[R_rda[g_ % NB]]) if NB == 2 else None
                kb.op("pe", lambda: nc.tensor.matmul(banks[7][:, 0:32], lhsT=LTm, rhs=da_tm, start=True, stop=True),
                      reads=[R_cst, R_small], writes=[bres[7]])
                kb.op("pe", lambda: nc.tensor.matmul(banks[7][:, 32:64], lhsT=ONm, rhs=da_tm, start=True, stop=True),
                      reads=[R_cst, R_small], writes=[bres[7]])
                kb.op("act", lambda: nc.scalar.copy(out=acum_sb, in_=banks[7][:, 0:32]), reads=[bres[7]], writes=[R_small])
                kb.op("act", lambda: nc.scalar.activation(out=ea, in_=banks[7][:, 0:32], func=AF.Exp), reads=[bres[7]], writes=[R_small])
                kb.op("act", lambda: nc.scalar.activation(out=cdv, in_=banks[7][:, 32:64], func=AF.Exp), reads=[bres[7]], writes=[R_small])
                kb.op("dve", lambda: nc.vector.tensor_tensor(out=dtmp, in0=banks[7][:, 32:64], in1=acum_sb, op=ALU.subtract),
                      reads=[bres[7], R_small], writes=[R_small])
                kb.op("act", lambda: nc.scalar.activation(out=dte, in_=dtmp, func=AF.Exp), reads=[R_small], writes=[R_small])
                kb.op("dve", lambda: nc.vector.tensor_tensor(out=w2, in0=dt_tm, in1=dte, op=ALU.mult), reads=[R_small], writes=[R_small])
                for hp in range(16):
                    kb.op("pe", lambda hp=hp: nc.tensor.transpose(out=banks[hp // 4][:, (hp % 4) * 128:(hp % 4 + 1) * 128],
                                                                  in_=xc[:, hp, cc], identity=ident),
                          reads=[R_xc[hp], R_cst], writes=[bres[hp // 4]])
                for q in range(4):
                    kb.op("dve", lambda q=q: nc.vector.tensor_tensor(
                        out=xdt[:, q * 512:(q + 1) * 512].rearrange("p (h e) -> p h e", e=HD),
                        in0=banks[q][:, :].rearrange("p (h e) -> p h e", e=HD),
                        in1=dt_tm[:, q * 8:(q + 1) * 8].unsqueeze(2).broadcast_to([128, 8, HD]), op=ALU.mult),
                        reads=[bres[q], R_small], writes=[R_xdt])
                def stXD(_):
                    for hf in range(2):
                        kb.op("pool", lambda hf=hf: nc.gpsimd.tensor_tensor(
                            out=xdtd[:, hf * 1024:(hf + 1) * 1024].rearrange("p (h e) -> p h e", e=HD),
                            in0=xdt[:, hf * 1024:(hf + 1) * 1024].rearrange("p (h e) -> p h e", e=HD),
                            in1=dte[:, hf * 16:(hf + 1) * 16].unsqueeze(2).broadcast_to([128, 16, HD]), op=ALU.mult),
                            reads=[R_xdt, R_small], writes=[R_xdtd])
                if sample:
                    stXD(0)
                b6 = banks[6][:, :].bitcast(BF16)
                for g in range(NG):
                    kb.op("pe", lambda g=g: nc.tensor.transpose(out=b6[:, g * 128:(g + 1) * 128], in_=BC[:, g, cc], identity=identb[:]),
                          reads=[R_bc[g], R_mc], writes=[bres[6]])
                kb.op("act", lambda: nc.scalar.copy(out=B_tm[:], in_=b6[:, 0:1024]), reads=[bres[6]], writes=[R_btm])
                if sample:
                    kb.dma("sp", t1_all[:], seqm_d[0:1, :].broadcast_to([128, NSAMP * 128]), kb.dsem("seqm"), writes=[R_t1all])
                    kb.op("dve", lambda: nc.vector.tensor_copy(out=seqmask[:].rearrange("p b t -> p (b t)"), in_=t1_all[:]),
                          reads=[R_t1all], writes=[R_seqm])
                    kb.op("dve", lambda: nc.vector.tensor_copy(
                        out=t1_all[:].rearrange("p (h e) -> p h e", e=HD), in_=da_tm.unsqueeze(2).broadcast_to([128, NH, HD])),
                        reads=[R_small], writes=[R_t1all])
                    for hp in range(16):
                        kb.op("pe", lambda hp=hp: nc.tensor.matmul(banks[6][:, hp * 16:(hp + 1) * 16],
                                                                   lhsT=t1_all[:, hp * 128:(hp + 1) * 128],
                                                                   rhs=cst[:, C_SEQIND:C_SEQIND + 16], start=True, stop=True),
                              reads=[R_t1all, R_cst], writes=[bres[6]])
                    kb.op("act", lambda: nc.scalar.activation(out=cd_fm[:], in_=banks[6][:, 0:256], func=AF.Exp),
                          reads=[bres[6]], writes=[R_cdfm])
                    b45 = [banks[4][:, :].bitcast(BF16), banks[5][:, :].bitcast(BF16)]
                    for b in range(NSAMP):
                        sb_ = b % 2
                        S0 = S0buf[sb_]
                        S0h = lambda hp, S0=S0: S0[:, hp // 2, hp % 2, :]
                        for two in range(2):
                            kb.dma("sp", S0[:, :, two, :], sst[b].rearrange("(k two h2) p n -> two (h2 p) k n", two=2, h2=2)[two],
                                   kb.dsem("s0in%d" % sb_), writes=[R_S0[sb_]])
                        for k2 in range(8):
                            kb.op("act", lambda k2=k2, S0=S0: nc.scalar.copy(
                                out=S0bf[:, k2 * 256:(k2 + 1) * 256].rearrange("p (two n) -> p two n", two=2), in_=S0[:, k2, :, :]),
                                reads=[R_S0[sb_]], writes=[R_S0bf])
                        for hp in range(16):
                            kb.op("pe", lambda hp=hp: nc.tensor.transpose(out=b45[hp // 8][:, (hp % 8) * 128:(hp % 8 + 1) * 128],
                                                                          in_=S0bf[:, hp * 128:(hp + 1) * 128], identity=identb[:]),
                                  reads=[R_S0bf, R_mc], writes=[bres[4 + hp // 8]])
                        for hf in range(2):
                            kb.op("dve", lambda hf=hf: nc.vector.tensor_copy(out=S0T[:, hf * 1024:(hf + 1) * 1024], in_=b45[hf][:, 0:1024]),
                                  reads=[bres[4 + hf]], writes=[R_S0T])
                        kb.op("dve", lambda b=b: nc.vector.tensor_tensor(
                            out=Cm[:], in0=BC[:, 8:16, cc], in1=seqmask[:, b, :].unsqueeze(1).broadcast_to([128, NG, 128]), op=ALU.mult),
                            reads=R_bc[8:16] + [R_seqm], writes=[R_Cm])
                        for g in range(NG):
                            kb.op("pe", lambda g=g, b=b: nc.tensor.matmul(
                                banks[g // 2][:, (g % 2) * 256:(g % 2 + 1) * 256], lhsT=Cm[:, g, :], rhs=S0T[:, g * 256:(g + 1) * 256],
                                start=(b == 0 and g % 2 == 0), stop=(b == NSAMP - 1 and g % 2 == 1)),
                                reads=[R_Cm, R_S0T], writes=[bres[g // 2]])
                        kb.op("dve", lambda b=b: nc.vector.tensor_scalar(out=Bm[:], in0=B_tm[:], scalar1=cst[:, C_SEQIND + b:C_SEQIND + b + 1],
                                                                         scalar2=None, op0=ALU.mult),
                              reads=[R_btm, R_cst], writes=[R_Bm])
                        for r4 in range(4):
                            bk = 6 + r4 % 2
                            for i in range(4):
                                hp = r4 * 4 + i
                                g = hp // 2
                                kb.op("pe", lambda hp=hp, i=i, g=g, bk=bk: nc.tensor.matmul(
                                    banks[bk][:, i * 128:(i + 1) * 128], lhsT=xdtd[:, hp * 128:(hp + 1) * 128],
                                    rhs=Bm[:, g * 128:(g + 1) * 128], start=True, stop=True),
                                    reads=[R_xdtd, R_Bm], writes=[bres[bk]])
                            for i in range(4):
                                hp = r4 * 4 + i
                                kb.op("dve", lambda hp=hp, i=i, bk=bk, b=b, S0h=S0h: nc.vector.scalar_tensor_tensor(
                                    out=S0h(hp), in0=S0h(hp), scalar=cd_fm[:, hp * 16 + b:hp * 16 + b + 1], op0=ALU.mult,
                                    in1=banks[bk][:, i * 128:(i + 1) * 128], op1=ALU.add),
                                    reads=[R_S0[sb_], R_cdfm, bres[bk]], writes=[R_S0[sb_]])
                        for two in range(2):
                            kb.dma("pool", ssm_s[b].rearrange("(k two q) n -> two q k n", two=2, q=128)[two], S0[:, :, two, :],
                                   kb.dsem("s0out%d" % sb_), reads=[R_S0[sb_]], is_out=True)
                    for q in range(4):
                        kb.op("dve", lambda q=q: nc.vector.tensor_tensor(
                            out=t1_all[:, q * 512:(q + 1) * 512].rearrange("p (h e) -> p h e", e=HD),
                            in0=banks[q][:, :].rearrange("p (h e) -> p h e", e=HD),
                            in1=ea[:, q * 8:(q + 1) * 8].unsqueeze(2).broadcast_to([128, 8, HD]), op=ALU.mult),
                            reads=[bres[q], R_small], writes=[R_t1all])
                for g in range(NG):
                    kb.op("pe", lambda g=g: nc.tensor.matmul(banks[4 + g // 4][:, (g % 4) * 128:(g % 4 + 1) * 128],
                                                             lhsT=BC[:, g, cc], rhs=BC[:, 8 + g, cc], start=True, stop=True),
                          reads=[R_bc[g], R_bc[8 + g]], writes=[bres[4 + g // 4]], lhs=[R_bc[g]])
                def stSTD(_):
                    kb.op("pool", lambda: nc.gpsimd.tensor_tensor(
                        out=ST.rearrange("p (h e) -> p h e", e=HD), in0=ST.rearrange("p (h e) -> p h e", e=HD),
                        in1=cdv.unsqueeze(2).broadcast_to([128, NH, HD]), op=ALU.mult),
                        reads=[R_ST, R_small], writes=[R_ST])

                def stAp(g):
                    rb = g % NB
                    kb.op("pool", lambda: nc.gpsimd.tensor_tensor(
                        out=rda[:, rb, :, :], in0=LTm.unsqueeze(1).broadcast_to([128, 4, 128]),
                        in1=da_tm[:, g * 4:(g + 1) * 4].unsqueeze(2).broadcast_to([128, 4, 128]), op=ALU.mult),
                        reads=[R_cst, R_small], writes=[R_rda[rb]])

                def stA(g):
                    rb = g % NB
                    mb = g % (2 * NB)
                    sbk = 6 + g % 2
                    kb.op("pe", lambda: nc.tensor.matmul(banks[sbk][:, :], lhsT=negib[:],
                                                         rhs=killb[:].rearrange("p a t -> p (a t)"), start=True, stop=False),
                          reads=[R_mc], writes=[bres[sbk]], lhs=[R_mc])
                    kb.op("pe", lambda: nc.tensor.matmul(banks[sbk][:, :], lhsT=Um, rhs=rda[:, rb, :, :].rearrange("p a t -> p (a t)"),
                                                         start=False, stop=True), reads=[R_cst, R_rda[rb]], writes=[bres[sbk]], lhs=[R_cst])
                    kb.op("act", lambda: nc.scalar.activation(out=E[:, rb, :], in_=banks[sbk][:, :], func=AF.Exp),
                          reads=[bres[sbk]], writes=[R_E[rb]])
                    cbk = 4 + g // 4
                    cb0 = (g % 4) * 128
                    kb.op("dve", lambda: nc.vector.tensor_tensor(
                        out=Mh[:, mb, :].rearrange("p (h t) -> p h t", h=4), in0=E[:, rb, :].rearrange("p (h t) -> p h t", h=4),
                        in1=banks[cbk][:, cb0:cb0 + 128].unsqueeze(1).broadcast_to([128, 4, 128]), op=ALU.mult),
                        reads=[R_E[rb], bres[cbk]], writes=[R_M[mb]])

                def stB1(q):
                    yb = q % NB
                    for h8 in range(8):
                        h = q * 8 + h8
                        mb = (2 * q + h8 // 4) % (2 * NB)
                        kb.op("pe", lambda h8=h8, h=h, mb=mb: nc.tensor.matmul(
                            banks[0][:, h8 * HD:(h8 + 1) * HD], lhsT=Mh[:, mb, (h8 % 4) * 128:(h8 % 4 + 1) * 128],
                            rhs=xdt[:, h * HD:(h + 1) * HD], start=True, stop=True),
                            reads=[R_M[mb], R_xdt], writes=[bres[0]], lhs=[R_M[mb]])
                    if sample:
                        kb.op("dve", lambda: nc.vector.tensor_tensor(out=y_tm[:, yb, :], in0=banks[0][:, :],
                                                                     in1=t1_all[:, q * 512:(q + 1) * 512], op=ALU.add),
                              reads=[bres[0], R_t1all], writes=[R_ytm[yb]])
                    else:
                        for gg in range(2):
                            g = 2 * q + gg
                            kb.op("pe", lambda gg=gg, g=g: nc.tensor.matmul(banks[1][:, gg * 256:(gg + 1) * 256], lhsT=BC[:, 8 + g, cc],
                                                                            rhs=STb[:, g * 256:(g + 1) * 256], start=True, stop=True),
                                  reads=[R_bc[8 + g], R_STb], writes=[bres[1]], lhs=[R_bc[8 + g]])
                        kb.op("dve", lambda: nc.vector.tensor_tensor(
                            out=t1[:].rearrange("p (h e) -> p h e", e=HD), in0=banks[1][:, :].rearrange("p (h e) -> p h e", e=HD),
                            in1=ea[:, q * 8:(q + 1) * 8].unsqueeze(2).broadcast_to([128, 8, HD]), op=ALU.mult),
                            reads=[bres[1], R_small], writes=[R_t1])
                        kb.op("dve", lambda: nc.vector.tensor_tensor(out=y_tm[:, yb, :], in0=banks[0][:, :], in1=t1[:], op=ALU.add),
                              reads=[bres[0], R_t1], writes=[R_ytm[yb]])

                def stB2(q):
                    yb = q % NB
                    for i in range(4):
                        kb.op("pe", lambda i=i: nc.tensor.transpose(out=banks[2][:, i * 128:(i + 1) * 128],
                                                                    in_=y_tm[:, yb, i * 128:(i + 1) * 128], identity=ident),
                              reads=[R_ytm[yb], R_cst], writes=[bres[2]])
                    for i in range(4):
                        hp = 4 * q + i
                        kb.op("dve", lambda hp=hp, i=i: nc.vector.scalar_tensor_tensor(
                            out=yf[:, yb, i, :], in0=xc[:, hp, cc], scalar=pfm[:, PF_D + hp:PF_D + hp + 1], op0=ALU.mult,
                            in1=banks[2][:, i * 128:(i + 1) * 128], op1=ALU.add),
                            reads=[R_xc[hp], R_pfm, bres[2]], writes=[R_yf[yb]])
                    kb.op("pool", lambda: nc.gpsimd.tensor_tensor(out=yz[:, yb, :, :], in0=yf[:, yb, :, :], in1=zs[:, 4 * q:4 * q + 4, cc],
                                                                  op=ALU.mult),
                          reads=[R_yf[yb]] + R_zs[4 * q:4 * q + 4], writes=[R_yz[yb]])
                    kb.op("pool", lambda: nc.gpsimd.tensor_tensor(out=ysq[:, yb, :, :], in0=yz[:, yb, :, :], in1=yz[:, yb, :, :],
                                                                  op=ALU.mult), reads=[R_yz[yb]], writes=[R_ysq[yb]])

                def stB2b(q):
                    yb = q % NB
                    for gg in range(2):
                        for i2 in range(2):
                            kb.op("pe", lambda gg=gg, i2=i2: nc.tensor.matmul(banks[3][:, gg * 128:(gg + 1) * 128], lhsT=ones_g[:],
                                                                              rhs=ysq[:, yb, 2 * gg + i2, :], start=(i2 == 0), stop=(i2 == 1)),
                                  reads=[R_ysq[yb], R_ones], writes=[bres[3]], lhs=[R_ones])
                    kb.op("act", lambda: nc.scalar.activation(out=banks[3][:, 0:256], in_=banks[3][:, 0:256], func=AF.Ln,
                                                              bias=eps_t[:, 0:1], scale=1.0),
                          reads=[bres[3], R_ones], writes=[bres[3]])
                    kb.op("act", lambda: nc.scalar.activation(out=banks[3][:, 0:256], in_=banks[3][:, 0:256], func=AF.Exp, scale=-0.5),
                          reads=[bres[3]], writes=[bres[3]])
                    for i in range(4):
                        hp = 4 * q + i
                        kb.op("dve", lambda hp=hp, i=i: nc.vector.scalar_tensor_tensor(
                            out=yg[:, hp, cc], in0=yz[:, yb, i, :], scalar=pfm[:, PF_SNG + hp:PF_SNG + hp + 1], op0=ALU.mult,
                            in1=banks[3][:, (i // 2) * 128:(i // 2 + 1) * 128], op1=ALU.mult),
                            reads=[R_yz[yb], R_pfm, bres[3]], writes=[R_yg[hp]])

                if NB == 2:
                    order = [("A", 0), ("A", 1), ("Ap", 2), ("Ap", 3), ("A", 2), ("A", 3), ("B1", 0), ("Ap", 4), ("Ap", 5),
                             ("A", 4), ("A", 5), ("B2", 0), ("B1", 1), ("Ap", 6), ("Ap", 7), ("XD", 0), ("STD", 0), ("A", 6), ("A", 7),
                             ("B2b", 0),
                             ("B2", 1), ("B1", 2), ("B2b", 1), ("B2", 2), ("B1", 3), ("B2b", 2), ("B2", 3), ("B2b", 3)]
                else:
                    order = []
                    for q in range(4):
                        order += [("Ap", 2 * q), ("A", 2 * q), ("Ap", 2 * q + 1), ("A", 2 * q + 1), ("B1", q), ("B2", q), ("B2b", q)]
                for (st_, a_) in order:
                    {"Ap": stAp, "A": stA, "B1": stB1, "B2": stB2, "B2b": stB2b, "STD": stSTD, "XD": stXD}[st_](a_)
                if not sample:
                    for g in range(NG):
                        kb.op("pe", lambda g=g: nc.tensor.matmul(banks[g // 2][:, (g % 2) * 256:(g % 2 + 1) * 256],
                                                                 lhsT=B_tm[:, g * 128:(g + 1) * 128],
                                                                 rhs=xdtd[:, g * 256:(g + 1) * 256], start=True, stop=True),
                              reads=[R_btm, R_xdtd], writes=[bres[g // 2]], lhs=[R_btm])
                    for q in range(4):
                        kb.op("dve", lambda q=q: nc.vector.tensor_tensor(out=ST[:, q * 512:(q + 1) * 512], in0=banks[q][:, :],
                                                                         in1=ST[:, q * 512:(q + 1) * 512], op=ALU.add),
                              reads=[bres[q], R_ST], writes=[R_ST])
                    kb.op("act", lambda: nc.scalar.copy(out=STb, in_=ST), reads=[R_ST], writes=[R_STb])
            for so in range(4):
                slot, sr = ws.get("sc", SL_SO + so)
                sv = slot[:].rearrange("p (m k c) -> p m k c", m=2, k=16)
                for mi in range(2):
                    c = so * 2 + mi
                    bk = mmb.next()
                    for kc in range(16):
                        kb.op("pe", lambda kc=kc, sv=sv, mi=mi, bk=bk: nc.tensor.matmul(
                            banks[bk][:, 0:T], lhsT=sv[:, mi, kc, :], rhs=yg[:, kc, 0:T], start=(kc == 0), stop=(kc == 15)),
                            reads=[sr, R_yg[kc]], writes=[bres[bk]], lhs=[sr])
                    resid_add(T, c, bk, l, 0, sample, ntmp, [R_nt0, R_nt1], c0=c0)
            if sample:
                R_cvt = Res("cvt")
                kb.fence_all(R_zs + R_xc + R_bc + R_yg + [R_scv], [R_cvt])
                arena_live.append(R_cvt)
                cvt = carve("cvt", 0, [128, CONVD], F32)
                for r in range(8):
                    bk = auxb.next()
                    for i in range(4):
                        j = r * 4 + i
                        kb.op("pe", lambda j=j, i=i, bk=bk: nc.tensor.transpose(out=banks[bk][0:48, i * 128:(i + 1) * 128],
                                                                                in_=cvs6[:, j, :], identity=ident),
                              reads=[R_cvs6, R_cst], writes=[bres[bk]])
                    kb.op("dve", lambda r=r, bk=bk: nc.vector.tensor_copy(out=cvt[0:48, r * 512:(r + 1) * 512], in_=banks[bk][0:48, :]),
                          reads=[bres[bk]], writes=[R_cvt])
                kb.dma("pool", cv_s.rearrange("b k f -> (b k) f"), cvt[0:48, :], kb.dsem("st_cvt"), reads=[R_cvt], is_out=True)

        def mamba_prompt_out():
            rr = new_phase(["so0", "so1", "so2", "so3", "cvo"])
            so = carve("so", 0, [128, 16, 128], F32)
            cvo = carve("cvo", 8192, [128, CONVD], F32)
            for hp in range(16):
                kb.op("pe", lambda hp=hp: nc.tensor.transpose(out=banks[hp // 4][:, (hp % 4) * 128:(hp % 4 + 1) * 128],
                                                              in_=ST[:, hp * 128:(hp + 1) * 128], identity=ident),
                      reads=[R_ST, R_cst], writes=[bres[hp // 4]])
            for q in range(4):
                kb.op("act" if q % 2 else "dve", (lambda q=q: nc.scalar.copy(out=so[:, q * 4:(q + 1) * 4, :].rearrange("p a n -> p (a n)"), in_=banks[q][:, :])) if q % 2
                      else (lambda q=q: nc.vector.tensor_copy(out=so[:, q * 4:(q + 1) * 4, :].rearrange("p a n -> p (a n)"), in_=banks[q][:, :])),
                      reads=[bres[q]], writes=[rr[q]])
            kb.dma("pool", ssm_p.rearrange("(hp q) n -> q hp n", q=128), so[:, :, :], kb.dsem("st_so"), reads=rr[0:4], is_out=True)
            for r in range(8):
                bk = 4 + r % 4
                for i in range(4):
                    j = r * 4 + i
                    kb.op("pe", lambda j=j, i=i, bk=bk: nc.tensor.transpose(out=banks[bk][0:3, i * 128:(i + 1) * 128],
                                                                            in_=cvst[:, j, :], identity=ident),
                          reads=[R_cvst, R_cst], writes=[bres[bk]])
                kb.op("dve", lambda r=r, bk=bk: nc.vector.tensor_copy(out=cvo[0:3, r * 512:(r + 1) * 512], in_=banks[bk][0:3, :]),
                      reads=[bres[bk]], writes=[rr[4]])
            kb.dma("pool", cv_p[:, :], cvo[0:3, :], kb.dsem("st_cvo"), reads=[rr[4]], is_out=True)

        def final_out(tiles):
            T = len(tiles) * 128
            rr = new_phase(["ft0", "ft1", "yo0", "yo1"] + ["fsq%d" % k for k in range(KC)] + ["yf%d" % k for k in range(KC)])
            R_tmp = rr[0:2]; R_yo = rr[2:4]; R_sq = rr[4:12]; R_yf = rr[12:20]
            y_fm = carve("y_fm", 0, [128, KC, 512], F32)
            sq_buf = carve("fsq", 16384, [128, KC, 512], BF16)
            yo = carve("yo", 24576, [128, 2, D], F32)
            bk = rms_rstd(T, [x_fm[:, kc, 0:T] for kc in range(KC)], R_x, sq_buf, R_sq, ones_bf)
            for kc in range(KC):
                kb.op("dve", lambda kc=kc: nc.vector.scalar_tensor_tensor(
                    out=y_fm[:, kc, 0:T], in0=x_fm[:, kc, 0:T], scalar=pfm[:, PF_FG + kc:PF_FG + kc + 1], op0=ALU.mult,
                    in1=banks[bk][:, 0:T], op1=ALU.mult), reads=[R_x[kc], bres[bk], R_pfm], writes=[R_yf[kc]])
            for ti, tg in enumerate(tiles):
                ob = ti % 2
                for hh in range(2):
                    b2 = auxb.next()
                    for q in range(4):
                        kc = hh * 4 + q
                        kb.op("pe", lambda kc=kc, q=q, b2=b2, ti=ti: nc.tensor.transpose(
                            out=banks[b2][:, q * 128:(q + 1) * 128], in_=y_fm[:, kc, ti * 128:(ti + 1) * 128],
                            identity=ident), reads=[R_yf[kc], R_cst], writes=[bres[b2]])
                    e = evq.next()
                    dst = yo[:, ob, hh * 512:(hh + 1) * 512]
                    if e == "act":
                        kb.op("act", lambda dst=dst, b2=b2: nc.scalar.copy(out=dst, in_=banks[b2][:, :]),
                              reads=[bres[b2]], writes=[R_yo[ob]])
                    else:
                        kb.op("dve", lambda dst=dst, b2=b2: nc.vector.tensor_copy(out=dst, in_=banks[b2][:, :]),
                              reads=[bres[b2]], writes=[R_yo[ob]])
                kb.dma("pool", y_out[tg * 128:(tg + 1) * 128, :], yo[:, ob, :], kb.dsem("st_yo%d" % ob), reads=[R_yo[ob]], is_out=True)

        ada_layer(0)
        emit_cast([1, 2], [R_modT])
        ada1_done = False
        blocks = [[0, 1, 2, 3], [4, 5, 6, 7], [8, 9, 10, 11], [12, 13, 14, 15], [16]]
        for bi, tiles in enumerate(blocks):
            if bi in skip:
                continue
            sample = (bi == 4)
            T = len(tiles) * 128
            if sample:
                kb.fence_all([R_ST, R_STb], [R_smod])
            if bi == 0 or bi == 4:
                gmlp_consts(1 if sample else 0)
            rr = new_phase(["xt0", "xt1"])
            xt = carve("xt", 0, [128, 2, D], F32)
            load_x(tiles, rr, xt)
            rr = new_phase(["nt0", "nt1"] + ["nsq%d" % k for k in range(KC)])
            tmp = carve("ntmp", 0, [128, 2, 512], F32)
            sq_buf = carve("nsq", 4096, [128, KC, 512], BF16)
            if bi == 0:
                dump("x_fm", x_fm[:, :, :], R_x, [128, KC, 512])
            norm_mod(T, 0, 0, sample, sq_buf, rr[2:10], tmp, rr[0:2])
            if bi == 0:
                emit_cast([3, 4], [R_h[7]])
                dump("h_fm", h_fm[:, :, :], R_h, [128, KC, 512], BF16)
                dump("Rt", Rt[:, :, :], [R_Rt], [128, 16, 128])
                dump("wstb", wstb[:, :], [R_wstb], [128, 1024], BF16)
            gmlp(tiles, 0, sample)
            if bi == 0:
                emit_cast([5, 6], [R_x[7]])
                dump("x1", x_fm[:, :, :], R_x, [128, KC, 512])
            mlp(T, 0, sample)
            if bi == 0:
                emit_cast([7, 8], [R_x[7]])
                dump("x2", x_fm[:, :, :], R_x, [128, KC, 512])
            if not ada1_done:
                ada_layer(1)
                ada1_done = True
            if stage >= 2 and (not sample or stage >= 3):
                if bi == 0:
                    mamba_consts()
                if bi == 0 or sample:
                    mamba_masks(sample)
                rr = new_phase(["nt0", "nt1"] + ["nsq%d" % k for k in range(KC)])
                tmp = carve("ntmp", 0, [128, 2, 512], F32)
                sq_buf = carve("nsq", 4096, [128, KC, 512], BF16)
                norm_mod(T, 1, 0, sample, sq_buf, rr[2:10], tmp, rr[0:2])
                if sample:
                    mamba(tiles, 0, True, False)
                else:
                    mamba(tiles[0:2], 0, False, False)
                    mamba(tiles[2:4], 256, False, bi == 3)
                    if bi == 3:
                        mamba_prompt_out()
            mlp(T, 1, sample)
            if bi == 0:
                dump("x3", x_fm[:, :, :], R_x, [128, KC, 512])
            final_out(tiles)

        if not kb.dry:
            for ds in kb.out_sems:
                nc.gpsimd.wait_ge(ds.handle, ds.total)

    kb.dry = True
    emit()
    kb.dry = False
    ws.start_real()
    emit()
    return nc


def _slabs_b(W, kc, mg):
    K, M = W.shape
    mc = M // 128
    a = W.reshape(kc, 128, mc // mg, mg, 128).transpose(2, 1, 3, 0, 4)
    return np.ascontiguousarray(a).reshape(mc // mg, 128, mg * kc * 128)


def _slabs_a(W, kc, nb=512):
    K, N = W.shape
    a = W.reshape(kc, 128, N // nb, nb).transpose(2, 1, 0, 3)
    return np.ascontiguousarray(a).reshape(N // nb, 128, kc * nb)


def _fm(v):
    return np.ascontiguousarray(v.reshape(-1, 128).T)


def _host_consts():
    k = np.arange(128)
    same = (k[:, None] // LS) == (k[None, :] // LS)
    lt = (k[:, None] <= k[None, :])
    u = (k[:, None] > k[None, :])
    cst = np.zeros((128, C_N), np.float32)
    cst[:, C_ID:C_ID + 128] = np.eye(128)
    cst[:, C_LT:C_LT + 128] = lt
    cst[:, C_U:C_U + 128] = u
    cst[:, C_LTB:C_LTB + 128] = lt & same
    cst[:, C_UB:C_UB + 128] = u & same
    cst[:, C_SAME:C_SAME + 128] = same
    cst[:, C_ONES:C_ONES + 128] = 1.0
    cst[:, C_KILL:C_KILL + 128] = ~lt
    cst[:, C_KILLB:C_KILLB + 128] = ~(lt & same)
    cst[:, C_NEGI:C_NEGI + 128] = -30000.0 * np.eye(128)
    cst[:, C_SEQIND:C_SEQIND + 16] = (k[:, None] // LS) == np.arange(16)[None, :]
    seqm = ((k[None, :] // LS) == np.arange(16)[:, None]).astype(np.float32).reshape(1, 16 * 128)
    return cst, seqm


_PROGRAM = None
_LAST = None


def kernel(**inputs):
    global _PROGRAM
    f32 = np.float32
    inp = {k: np.asarray(v) for k, v in inputs.items()}
    slabs = [
        _slabs_a(inp["gm_w_in"][0][:, DGM:], KC),
        _slabs_b(inp["gm_w_in"][0][:, :DGM], KC, 4),
        _slabs_b(inp["gm_w_out"][0], 16, 2),
        _slabs_b(inp["mlp_w1"][0], KC, 4),
        _slabs_b(inp["mlp_w2"][0], 32, 1),
        _slabs_b(inp["ssm_w_in"][0][:, :6144], KC, 4),
        _slabs_b(inp["ssm_w_out"][0], 16, 2),
        _slabs_b(inp["mlp_w1"][1], KC, 4),
        _slabs_b(inp["mlp_w2"][1], 32, 1),
    ]
    w_all = np.ascontiguousarray(np.concatenate(slabs, axis=0).astype(f32))
    assert w_all.shape == (NSLAB, 128, 4096)
    w_ada = np.ascontiguousarray(np.concatenate([_slabs_a(inp["ada_w"][l], KC, 256) for l in range(2)], axis=0).astype(f32))
    pfm = np.zeros((128, PF_N), f32)
    for l in range(2):
        pfm[:, PF_ADAB + 48 * l:PF_ADAB + 48 * (l + 1)] = _fm(inp["ada_b"][l])
        pfm[:, PF_N1G + 8 * l:PF_N1G + 8 * (l + 1)] = _fm(inp["norm1_g"][l])
        pfm[:, PF_N2G + 8 * l:PF_N2G + 8 * (l + 1)] = _fm(inp["norm2_g"][l])
    pfm[:, PF_FG:PF_FG + 8] = _fm(inp["final_g"])
    pfm[:, PF_LNG:PF_LNG + 16] = _fm(inp["gm_ln_g"][0])
    pfm[:, PF_LNB:PF_LNB + 16] = _fm(inp["gm_ln_b"][0])
    for kk in range(4):
        pfm[:, PF_CW + 32 * kk:PF_CW + 32 * (kk + 1)] = _fm(inp["ssm_conv_w"][0][kk])
    pfm[:, PF_CB:PF_CB + 32] = _fm(inp["ssm_conv_b"][0])
    pfm[:, PF_SNG:PF_SNG + 16] = _fm(inp["ssm_norm_g"][0])
    pfm[:, PF_D:PF_D + 16] = _fm(np.repeat(inp["ssm_d"][0], HD))
    p32 = np.stack([inp["ssm_dt_bias"][0], inp["ssm_a_log"][0]], axis=1).astype(f32)
    cst, seqm = _host_consts()
    lngb = np.stack([inp["gm_ln_g"][0], inp["gm_ln_b"][0]], axis=0).astype(f32)
    ws_ = inp["gm_w_s"][0]
    wst_p = np.transpose(ws_, (2, 0, 1)).reshape(128, NG * 128)
    ws8 = np.tile(ws_[:, :LS, :LS], (1, NSAMP, NSAMP))
    wst_s = np.transpose(ws8, (2, 0, 1)).reshape(128, NG * 128)
    wst = np.ascontiguousarray(np.stack([wst_p, wst_s], axis=0).astype(f32))
    bs_p = inp["gm_b_s"][0].reshape(NG * 128)
    bs_s = np.tile(inp["gm_b_s"][0][:, :LS], (1, NSAMP)).reshape(NG * 128)
    bs = np.ascontiguousarray(np.stack([bs_p, bs_s], axis=0).astype(f32))
    wdt = np.ascontiguousarray(inp["ssm_w_in"][0][:, 6144:6176].reshape(KC, 128, NH).transpose(1, 0, 2).reshape(128, KC * NH).astype(f32))

    in_maps = []
    for c in range(NCORES):
        xin = np.concatenate([inp["x_prompt"][c], inp["x_sample"][c * NSAMP:(c + 1) * NSAMP].reshape(NSAMP * LS, D)], axis=0)
        cin = np.concatenate([inp["c_prompt"][c:c + 1], inp["c_sample"][c * NSAMP:(c + 1) * NSAMP]], axis=0)
        in_maps.append({
            "xin": np.ascontiguousarray(xin, dtype=f32), "cin": np.ascontiguousarray(cin, dtype=f32),
            "sst": np.ascontiguousarray(inp["state_ssm"][0, c * NSAMP:(c + 1) * NSAMP], dtype=f32),
            "scv": np.ascontiguousarray(inp["state_conv"][0, c * NSAMP:(c + 1) * NSAMP].reshape(NSAMP * 3, CONVD), dtype=f32),
            "w_all": w_all, "w_ada": w_ada, "pfm": pfm, "p32": p32, "cst": cst, "lngb": lngb, "wst": wst, "bs": bs,
            "wdt": wdt, "seqm": seqm,
        })
    if _PROGRAM is None:
        _PROGRAM = build_program()
    res = run_bass_kernel_spmd(_PROGRAM, in_maps, core_ids=list(range(NCORES)))
    R = res.results
    global _LAST
    _LAST = R
    y_prompt = np.stack([R[c]["y_out"][:SEQ] for c in range(NCORES)], axis=0)
    y_sample = np.concatenate([R[c]["y_out"][SEQ:].reshape(NSAMP, LS, D) for c in range(NCORES)], axis=0)
    gm_v_prompt = np.stack([R[c]["gmv_p"] for c in range(NCORES)], axis=0)[None]
    gm_v_sample = np.concatenate([R[c]["gmv_s"].reshape(NSAMP, LS, DGM) for c in range(NCORES)], axis=0)[None]
    ssm_p = np.stack([R[c]["ssm_p"].reshape(NH, HD, DST) for c in range(NCORES)], axis=0)[None]
    cv_p = np.stack([R[c]["cv_p"] for c in range(NCORES)], axis=0)[None]
    ssm_s = np.concatenate([R[c]["ssm_s"].reshape(NSAMP, NH, HD, DST) for c in range(NCORES)], axis=0)[None]
    cv_s = np.concatenate([R[c]["cv_s"] for c in range(NCORES)], axis=0)[None]
    return tuple(np.ascontiguousarray(a, dtype=f32) for a in
                 (y_prompt, y_sample, gm_v_prompt, gm_v_sample, ssm_p, cv_p, ssm_s, cv_s))
```

```python
import math
import numpy as np
import concourse.bass as bass
import concourse.mybir as mybir
from concourse.bass_utils import run_bass_kernel_spmd

F32 = mybir.dt.float32
BF16 = mybir.dt.bfloat16
AF = mybir.ActivationFunctionType
ALU = mybir.AluOpType

NCORES = 8
D = 1024
KC = 8
SEQ = 2048
NSAMP = 16
LS = 8
DGM = 2048
DFF = 4096
DIN = 2048
NH = 32
HD = 64
NG = 8
DST = 128
CONVD = 4096
EPS = 1e-6
NTOK = SEQ + NSAMP * LS

SL_GV, SL_GU, SL_GO, SL_M1_0, SL_M2_0, SL_SI, SL_SO, SL_M1_1, SL_M2_1 = 0, 4, 8, 12, 20, 28, 40, 44, 52
NSLAB = 60
SLAB_GROUPS = [(0, 4), (4, 8), (8, 12), (12, 20), (20, 28), (28, 40), (40, 44), (44, 52), (52, 60)]

PF_ADAB, PF_N1G, PF_N2G, PF_FG, PF_LNG, PF_LNB, PF_CW, PF_CB, PF_SNG, PF_D = 0, 96, 112, 128, 136, 152, 168, 296, 328, 344
PF_N = 360
C_ID, C_LT, C_U, C_LTB, C_UB, C_SAME, C_ONES, C_KILL, C_KILLB, C_NEGI, C_SEQIND = 0, 128, 256, 384, 512, 640, 768, 896, 1024, 1152, 1280
C_N = 1296


class Res:
    __slots__ = ("name", "w", "r")

    def __init__(self, name):
        self.name = name
        self.w = None
        self.r = {}


class DSem:
    def __init__(self, handle):
        self.handle = handle
        self.total = 0


class KB:
    def __init__(self):
        self.nc = bass.Bass("TRN2", target_bir_lowering=False)
        nc = self.nc
        self.dry = False
        self.eng = {"pe": nc.tensor, "act": nc.scalar, "dve": nc.vector, "pool": nc.gpsimd, "sp": nc.sync}
        self.sem = {e: nc.alloc_semaphore("s_" + e) for e in ("pe", "act", "dve", "pool")}
        self.cnt = {e: 0 for e in ("pe", "act", "dve", "pool")}
        self.seen = {e: {} for e in self.eng}
        self.snap = {e: {} for e in ("pe", "act", "dve", "pool")}
        self._sb = {}
        self._res = {}
        self._ds = {}
        self.out_sems = []

    def sb(self, name, shape, dtype):
        if name not in self._sb:
            self._sb[name] = self.nc.alloc_sbuf_tensor("sb_" + name, list(shape), dtype).ap()
        return self._sb[name]

    def res(self, name):
        if name not in self._res:
            self._res[name] = Res(name)
        return self._res[name]

    def dsem(self, name):
        if name not in self._ds:
            self._ds[name] = DSem(self.nc.alloc_semaphore("d_" + name))
        return self._ds[name]

    def _wait(self, eng, key, val):
        if key == "pe" and eng == "pe":
            return
        if self.seen[eng].get(key, 0) >= val:
            return
        h = self.sem[key] if isinstance(key, str) else key.handle
        self.eng[eng].wait_ge(h, val)
        self._learn(eng, key, val)

    def _learn(self, eng, key, val):
        se = self.seen[eng]
        if se.get(key, 0) < val:
            se[key] = val
        if isinstance(key, str):
            sn = self.snap[key].get(val)
            if sn:
                for k2, v2 in sn.items():
                    if se.get(k2, 0) < v2:
                        se[k2] = v2

    def _deps(self, eng, reads, writes, lhs=None):
        need = {}
        for r in reads:
            if r.w is not None:
                k, v = r.w
                if need.get(k, 0) < v:
                    need[k] = v
        for w in writes:
            if w.w is not None:
                k, v = w.w
                if need.get(k, 0) < v:
                    need[k] = v
            for k, v in w.r.items():
                if need.get(k, 0) < v:
                    need[k] = v
        todo = []
        for k, v in need.items():
            if k == "pe" and eng == "pe":
                continue
            if self.seen[eng].get(k, 0) >= v:
                continue
            todo.append((k, v))
        attach = None
        if todo and eng != "pe":
            attach = todo.pop()
        elif todo and lhs is not None:
            lneed = {}
            for r in lhs:
                if r.w is not None:
                    k, v = r.w
                    if lneed.get(k, 0) < v:
                        lneed[k] = v
            for i in range(len(todo) - 1, -1, -1):
                k, v = todo[i]
                if lneed.get(k, 0) <= self.seen[eng].get(k, 0):
                    attach = todo.pop(i)
                    break
        for k, v in todo:
            self._wait(eng, k, v)
        return attach

    def _attach(self, eng, ins, attach):
        if attach is not None:
            k, v = attach
            h = self.sem[k] if isinstance(k, str) else k.handle
            ins._wait_ge(h, v)
            self._learn(eng, k, v)

    def op(self, eng, fn, reads=(), writes=(), lhs=None):
        if self.dry:
            return
        attach = self._deps(eng, reads, writes, lhs)
        ins = fn()
        self._attach(eng, ins, attach)
        self.cnt[eng] += 1
        n = self.cnt[eng]
        ins.then_inc(self.sem[eng], 1)
        self.snap[eng][n] = dict(self.seen[eng])
        for r in reads:
            r.r[eng] = n
        for w in writes:
            w.w = (eng, n)
            w.r = {}

    def dma(self, q, out, in_, ds, reads=(), writes=(), is_out=False, **kw):
        if self.dry:
            return
        if is_out and ds not in self.out_sems:
            self.out_sems.append(ds)
        attach = self._deps(q, reads, writes)
        ins = self.eng[q].dma_start(out=out, in_=in_, **kw)
        self._attach(q, ins, attach)
        ds.total += 16
        ins.then_inc(ds.handle, 16)
        for r in reads:
            r.r[ds] = ds.total
        for w in writes:
            w.w = (ds, ds.total)
            w.r = {}

    def fence_all(self, resources, into):
        acc = {}
        for r in resources:
            if r.w is not None:
                k, v = r.w
                acc[k] = max(acc.get(k, 0), v)
            for k, v in r.r.items():
                acc[k] = max(acc.get(k, 0), v)
        for r in into:
            for k, v in acc.items():
                r.r[k] = max(r.r.get(k, 0), v)


class WStream:
    def __init__(self, kb, nslots, depth, wsc, wsc_res, ada):
        self.kb = kb
        self.nslots = nslots
        self.depth = depth
        self.wsc = wsc
        self.wsc_res = wsc_res
        self.ada = ada
        self.slots = [kb.sb("wslot%d" % i, [128, 4096], BF16) for i in range(nslots)]
        self.sres = [kb.res("wslot%d" % i) for i in range(nslots)]
        self.sds = [kb.dsem("wslot%d" % i) for i in range(nslots)]
        self.plan = []
        self.pos = 0
        self.issued = 0

    def start_real(self):
        self.pos = 0
        self.issued = 0
        self.released = 0

    def _issue(self, i):
        kb = self.kb
        kind, idx = self.plan[i]
        s = i % self.nslots
        if kind == "sc":
            assert self.wsc_res[idx].w is not None, ("weight slab loaded before its cast was emitted", idx)
            kb.dma("sp", self.slots[s][:], self.wsc[idx], self.sds[s], reads=[self.wsc_res[idx]], writes=[self.sres[s]])
        else:
            kb.dma("sp", self.slots[s][:].bitcast(F32), self.ada[idx], self.sds[s], reads=[], writes=[self.sres[s]])

    def get(self, kind, idx, hold=False):
        if self.kb.dry:
            self.plan.append((kind, idx))
            return self.slots[0], self.sres[0]
        i = self.pos
        self.pos += 1
        assert self.plan[i] == (kind, idx), (i, self.plan[i], kind, idx)
        if not hold:
            self.released = i
        lim = min(i + self.depth + 1, self.released + self.nslots, len(self.plan))
        while self.issued < lim:
            self._issue(self.issued)
            self.issued += 1
        return self.slots[i % self.nslots], self.sres[i % self.nslots]


class Rot:
    def __init__(self, items):
        self.items = items
        self.i = 0

    def next(self):
        it = self.items[self.i % len(self.items)]
        self.i += 1
        return it


def build_program(stage=99, dbg=False, skip=()):
    kb = KB()
    nc = kb.nc

    def din(name, shape, dt=F32):
        return nc.dram_tensor(name, list(shape), dt, kind="ExternalInput").ap()

    def dout(name, shape, dt=F32):
        return nc.dram_tensor(name, list(shape), dt, kind="ExternalOutput").ap()

    xin = din("xin", [NTOK, D])
    cin = din("cin", [1 + NSAMP, D])
    sst = din("sst", [NSAMP, NH, HD, DST])
    scv = din("scv", [NSAMP * 3, CONVD])
    w_all = din("w_all", [NSLAB, 128, 4096])
    w_ada = din("w_ada", [48, 128, 2048])
    pfm_d = din("pfm", [128, PF_N])
    p32_d = din("p32", [32, 2])
    cst_d = din("cst", [128, C_N])
    lngb_d = din("lngb", [2, DGM])
    wst_d = din("wst", [2, 128, NG * 128])
    bs_d = din("bs", [2, NG * 128])
    wdt_d = din("wdt", [128, KC * NH])
    seqm_d = din("seqm", [1, NSAMP * 128])

    y_out = dout("y_out", [NTOK, D])
    gmv_p = dout("gmv_p", [128, DGM])
    gmv_s = dout("gmv_s", [128, DGM])
    ssm_p = dout("ssm_p", [NH * HD, DST])
    cv_p = dout("cv_p", [3, CONVD])
    ssm_s = dout("ssm_s", [NSAMP, NH * HD, DST])
    cv_s = dout("cv_s", [NSAMP, 3, CONVD])
    wsc_t = nc.dram_tensor("wsc", [NSLAB, 128, 4096], BF16, kind="Internal").ap()

    cst = kb.sb("cst", [128, C_N], F32)
    pfm = kb.sb("pfmt", [128, PF_N], F32)
    p32 = kb.sb("p32t", [32, 2], F32)
    modT = kb.sb("modT", [128, 96, 17], F32)
    modA = kb.sb("modA", [128, 4, 8, 17], F32)
    x_fm = kb.sb("x_fm", [128, KC, 512], F32)
    h_fm = kb.sb("h_fm", [128, KC, 512], BF16)
    ones_bf = kb.sb("ones_bf", [128, 128], BF16)
    ones_g = kb.sb("ones_g", [128, 128], BF16)
    wstb = kb.sb("wstb", [128, NG * 128], BF16)
    Rt = kb.sb("Rt", [128, 16, 128], F32)
    stsm = kb.sb("stsm", [128, 3072], F32)
    smod = stsm[:, :].rearrange("p (a k t) -> p a k t", a=3, k=KC)
    ST = stsm[:, 0:2048]
    STb = stsm[:, 2048:3072].bitcast(BF16)
    wdt_bf = kb.sb("wdt_bf", [128, KC, NH], BF16)
    identb = kb.sb("identb", [128, 128], BF16)
    negib = kb.sb("negib", [128, 128], BF16)
    killb = kb.sb("killb", [128, 4, 128], BF16)
    halo = kb.sb("halo", [128, 32, 3], BF16)
    cvst = kb.sb("cvst", [128, 32, 3], F32)
    dtda_fm = kb.sb("dtda_fm", [32, 2, 256], F32)
    acol = kb.sb("acol", [32, 1], F32)
    ARENA_B = 84 * 1024
    arena = kb.sb("arena", [128, ARENA_B // 4], F32)
    banks = [nc.alloc_psum_tensor("bank%d" % i, [128, 512], F32).ap() for i in range(8)]
    bres = [kb.res("bank%d" % i) for i in range(8)]

    R_cst, R_pfm, R_p32 = kb.res("cst"), kb.res("pfm"), kb.res("p32")
    R_modT, R_modA = kb.res("modT"), kb.res("modA")
    R_x = [kb.res("x_fm%d" % k) for k in range(KC)]
    R_h = [kb.res("h_fm%d" % k) for k in range(KC)]
    R_ones = kb.res("ones")
    R_wstb, R_Rt, R_smod = kb.res("wstb"), kb.res("Rt"), kb.res("smod")
    wsc_res = [kb.res("wsc%d" % i) for i in range(NSLAB)]
    ds_cst = kb.dsem("cst")
    ds_out = kb.dsem("out")

    ws = WStream(kb, 6, 4, [wsc_t[i] for i in range(NSLAB)], wsc_res, [w_ada[i] for i in range(48)])

    arena_live = []
    dbg_names = []

    def dump(name, ap, reads, shape, dt=F32):
        if not dbg or kb.dry or name in dbg_names:
            return
        dbg_names.append(name)
        t = nc.dram_tensor("dbg_" + name, list(shape), dt, kind="ExternalOutput").ap()
        kb.dma("pool", t, ap, kb.dsem("dbg_" + name), reads=reads, is_out=True)

    def carve(name, off, shape, dtype):
        n = int(np.prod(shape[1:]))
        nbytes = n * (4 if dtype == F32 else 2)
        assert off % 4 == 0 and off + nbytes <= ARENA_B, (name, off, nbytes)
        v = arena[:, off // 4:(off + nbytes) // 4]
        if dtype != F32:
            v = v.bitcast(dtype)
        if len(shape) == 3:
            v = v.rearrange("p (a b) -> p a b", b=shape[2])
        elif len(shape) == 4:
            v = v.rearrange("p (a b c) -> p a b c", b=shape[2], c=shape[3])
        return v

    def new_phase(names):
        nonlocal arena_live
        fresh = [Res(n) for n in names]
        kb.fence_all(arena_live, fresh)
        arena_live = fresh
        return fresh

    def emit_cast(groups, gate):
        if kb.dry:
            return
        for gi in groups:
            (s0, s1) = SLAB_GROUPS[gi]
            ds = kb.dsem("cast%d" % gi)
            for s in range(s0, s1, 2):
                kb.dma("pool", wsc_t[s:s + 2].rearrange("s p (a e) -> (s p) a e", e=2048),
                       w_all[s:s + 2].rearrange("s p (a e) -> (s p) a e", e=2048), ds,
                       reads=gate, writes=[wsc_res[s], wsc_res[s + 1]])
            for s in range(s0, s1):
                wsc_res[s].w = (ds, ds.total)

    def emit():
        mmb = Rot([0, 1, 2, 3])
        auxb = Rot([4, 5, 6, 7])
        evq = Rot(["act", "dve"])

        if not kb.dry:
            emit_cast([0], [])
            for (dst, src, r) in ((cst, cst_d, R_cst), (pfm, pfm_d, R_pfm), (p32, p32_d, R_p32)):
                kb.dma("sp", dst[:], src[:], ds_cst, writes=[r])
            for r in (R_cst, R_pfm, R_p32):
                r.w = (ds_cst, ds_cst.total)

        ident = cst[:, C_ID:C_ID + 128]

        kb.op("dve", lambda: nc.vector.memset(ones_bf[:], 1.0 / 1024.0), writes=[R_ones])
        kb.op("dve", lambda: nc.vector.memset(ones_g[:], 1.0 / 256.0), writes=[R_ones])

        (R_ct,) = new_phase(["c_tm"])
        R_cT = kb.res("cT")
        c_tm = carve("c_tm", 0, [128, D], F32)
        cT = kb.sb("cT", [128, KC, 17], F32)
        ds_c = kb.dsem("cin")
        kb.dma("sp", c_tm[0:17, :], cin[:], ds_c, writes=[R_ct])
        kb.op("act", lambda: nc.scalar.activation(out=c_tm[0:17, :], in_=c_tm[0:17, :], func=AF.Silu),
              reads=[R_ct], writes=[R_ct])
        bk = auxb.next()
        for kc in range(KC):
            kb.op("pe", lambda kc=kc: nc.tensor.transpose(out=banks[bk][:, kc * 17:(kc + 1) * 17],
                                                          in_=c_tm[0:17, kc * 128:(kc + 1) * 128],
                                                          identity=ident[0:17, 0:17]),
                  reads=[R_ct, R_cst], writes=[bres[bk]])
        kb.op("dve", lambda: nc.vector.tensor_copy(out=cT[:].rearrange("p a b -> p (a b)"), in_=banks[bk][:, 0:KC * 17]),
              reads=[bres[bk]], writes=[R_cT])

        def ada_layer(l):
            (R_msm,) = new_phase(["mod_sm"])
            mod_sm = carve("mod_sm", 8192, [128, 6 * D], F32)
            for sl in range(24):
                slot, sr = ws.get("ada", l * 24 + sl)
                sv = slot[:].bitcast(F32).rearrange("p (k c) -> p k c", k=KC)
                if sl % 2 == 0:
                    bk = auxb.next()
                for kc in range(KC):
                    kb.op("pe", lambda sv=sv, kc=kc, bk=bk, sl=sl: nc.tensor.matmul(
                        banks[bk][0:17, (sl % 2) * 256:(sl % 2 + 1) * 256], lhsT=cT[:, kc, :], rhs=sv[:, kc, :],
                        start=(kc == 0), stop=(kc == KC - 1)), reads=[sr, R_cT], writes=[bres[bk]])
                if sl % 2 == 1:
                    e = evq.next()
                    dst = mod_sm[0:17, (sl - 1) * 256:(sl + 1) * 256]
                    if e == "act":
                        kb.op("act", lambda dst=dst, bk=bk: nc.scalar.copy(out=dst, in_=banks[bk][0:17, :]),
                              reads=[bres[bk]], writes=[R_msm])
                    else:
                        kb.op("dve", lambda dst=dst, bk=bk: nc.vector.tensor_copy(out=dst, in_=banks[bk][0:17, :]),
                              reads=[bres[bk]], writes=[R_msm])
            for half in range(2):
                bk = auxb.next()
                for mi in range(24):
                    m = half * 24 + mi
                    kb.op("pe", lambda m=m, mi=mi, bk=bk: nc.tensor.transpose(
                        out=banks[bk][:, mi * 17:(mi + 1) * 17], in_=mod_sm[0:17, m * 128:(m + 1) * 128],
                        identity=ident[0:17, 0:17]), reads=[R_msm, R_cst], writes=[bres[bk]])
                m0 = l * 48 + half * 24
                kb.op("dve", lambda m0=m0, bk=bk: nc.vector.tensor_tensor(
                    out=modT[:, m0:m0 + 24, :],
                    in0=banks[bk][:, 0:24 * 17].rearrange("p (m s) -> p m s", s=17),
                    in1=pfm[:, PF_ADAB + m0:PF_ADAB + m0 + 24].unsqueeze(2).broadcast_to([128, 24, 17]),
                    op=ALU.add), reads=[bres[bk], R_pfm], writes=[R_modT])
            for sub in range(2):
                sc0 = l * 48 + sub * 24 + 8
                gcol = (PF_N1G if sub == 0 else PF_N2G) + l * 8
                kb.op("dve", lambda sc0=sc0, gcol=gcol, sub=sub: nc.vector.scalar_tensor_tensor(
                    out=modA[:, l * 2 + sub, :, :], in0=modT[:, sc0:sc0 + 8, :], scalar=1.0, op0=ALU.add,
                    in1=pfm[:, gcol:gcol + 8].unsqueeze(2).broadcast_to([128, 8, 17]), op1=ALU.mult),
                    reads=[R_modT, R_pfm], writes=[R_modA])

            if l == 0:
                dump("modT", modT[:, 0:48, :], [R_modT], [128, 48, 17])

        def mod_cols(l, sub):
            b0 = l * 48 + sub * 24
            A = lambda kc: modA[:, l * 2 + sub, kc, 0:1]
            B = lambda kc: modT[:, b0 + kc, 0:1]
            G = lambda kc: modT[:, b0 + 16 + kc, 0:1]
            return A, B, G

        def build_smod(l, sub):
            b0 = l * 48 + sub * 24
            srcs = [modA[:, l * 2 + sub, :, 1:17], modT[:, b0:b0 + 8, 1:17], modT[:, b0 + 16:b0 + 24, 1:17]]
            for i, s in enumerate(srcs):
                kb.op("dve", lambda i=i, s=s: nc.vector.tensor_copy(
                    out=smod[:, i, :, :].rearrange("p k (b t) -> p k b t", t=LS),
                    in_=s.unsqueeze(3).broadcast_to([128, KC, NSAMP, LS])),
                    reads=[R_modT, R_modA], writes=[R_smod])

        def load_x(tiles, R_xt, xt):
            for ti, tg in enumerate(tiles):
                b = ti % 2
                kb.dma("sp", xt[:, b, :], xin[tg * 128:(tg + 1) * 128, :], kb.dsem("xt%d" % b), writes=[R_xt[b]])
                for hh in range(2):
                    bk = auxb.next()
                    for q in range(4):
                        kc = hh * 4 + q
                        kb.op("pe", lambda b=b, kc=kc, q=q, bk=bk: nc.tensor.transpose(
                            out=banks[bk][:, q * 128:(q + 1) * 128], in_=xt[:, b, kc * 128:(kc + 1) * 128],
                            identity=ident), reads=[R_xt[b], R_cst], writes=[bres[bk]], lhs=[R_xt[b]])
                    e = evq.next()
                    dst = x_fm[:, hh * 4:hh * 4 + 4, ti * 128:(ti + 1) * 128]
                    src = banks[bk][:].rearrange("p (q t) -> p q t", t=128)
                    if e == "act":
                        kb.op("act", lambda dst=dst, src=src: nc.scalar.copy(out=dst, in_=src),
                              reads=[bres[bk]], writes=R_x[hh * 4:hh * 4 + 4])
                    else:
                        kb.op("dve", lambda dst=dst, src=src: nc.vector.tensor_copy(out=dst, in_=src),
                              reads=[bres[bk]], writes=R_x[hh * 4:hh * 4 + 4])

        def rms_rstd(T, srcs, src_res, sq_buf, R_sq, ones_t, eps_scale=1.0):
            n = len(srcs)
            for i in range(n):
                kb.op("act", lambda i=i: nc.scalar.activation(out=sq_buf[:, i, 0:T], in_=srcs[i], func=AF.Square),
                      reads=[src_res[i]], writes=[R_sq[i]])
            bk = auxb.next()
            for i in range(n):
                kb.op("pe", lambda i=i: nc.tensor.matmul(banks[bk][:, 0:T], lhsT=ones_t[:], rhs=sq_buf[:, i, 0:T],
                                                         start=(i == 0), stop=(i == n - 1)),
                      reads=[R_sq[i], R_ones], writes=[bres[bk]], lhs=[R_ones])
            kb.op("act", lambda: nc.scalar.activation(out=banks[bk][:, 0:T], in_=banks[bk][:, 0:T], func=AF.Ln,
                                                      bias=eps_t[:, 0:1], scale=1.0),
                  reads=[bres[bk], R_ones], writes=[bres[bk]])
            kb.op("act", lambda: nc.scalar.activation(out=banks[bk][:, 0:T], in_=banks[bk][:, 0:T], func=AF.Exp, scale=-0.5),
                  reads=[bres[bk]], writes=[bres[bk]])
            return bk

        eps_t = kb.sb("eps_t", [128, 1], F32)
        mhalf = kb.sb("mhalf", [128, 1], F32)
        kb.op("dve", lambda: nc.vector.memset(eps_t[:], EPS), writes=[R_ones])
        kb.op("dve", lambda: nc.vector.memset(mhalf[:], -0.5), writes=[R_ones])

        def norm_mod(T, l, sub, sample, sq_buf, R_sq, tmp, R_tmp):
            bk = rms_rstd(T, [x_fm[:, kc, 0:T] for kc in range(KC)], R_x, sq_buf, R_sq, ones_bf)
            if not sample:
                A, B, G = mod_cols(l, sub)
                for kc in range(KC):
                    t = tmp[:, kc % 2, 0:T]
                    kb.op("dve", lambda kc=kc, t=t: nc.vector.scalar_tensor_tensor(
                        out=t, in0=x_fm[:, kc, 0:T], scalar=A(kc), op0=ALU.mult, in1=banks[bk][:, 0:T], op1=ALU.mult),
                        reads=[R_x[kc], bres[bk], R_modA], writes=[R_tmp[kc % 2]])
                    kb.op("act", lambda kc=kc, t=t: nc.scalar.activation(
                        out=h_fm[:, kc, 0:T], in_=t, func=AF.Identity, bias=B(kc), scale=1.0),
                        reads=[R_tmp[kc % 2], R_modT], writes=[R_h[kc]])
            else:
                build_smod(l, sub)
                for kc in range(KC):
                    t = tmp[:, kc % 2, 0:T]
                    kb.op("dve", lambda kc=kc, t=t: nc.vector.tensor_tensor(
                        out=t, in0=x_fm[:, kc, 0:T], in1=banks[bk][:, 0:T], op=ALU.mult),
                        reads=[R_x[kc], bres[bk]], writes=[R_tmp[kc % 2]])
                    kb.op("dve", lambda kc=kc, t=t: nc.vector.tensor_tensor(
                        out=t, in0=t, in1=smod[:, 0, kc, :], op=ALU.mult),
                        reads=[R_tmp[kc % 2], R_smod], writes=[R_tmp[kc % 2]])
                    kb.op("dve", lambda kc=kc, t=t: nc.vector.tensor_tensor(
                        out=h_fm[:, kc, 0:T], in0=t, in1=smod[:, 1, kc, :], op=ALU.add),
                        reads=[R_tmp[kc % 2], R_smod], writes=[R_h[kc]])

        def resid_add(T, c, bk, l, sub, sample, tmp, R_tmp, c0=0):
            xs = x_fm[:, c, c0:c0 + T]
            if not sample:
                A, B, G = mod_cols(l, sub)
                kb.op("dve", lambda: nc.vector.scalar_tensor_tensor(
                    out=xs, in0=banks[bk][:, 0:T], scalar=G(c), op0=ALU.mult, in1=xs, op1=ALU.add),
                    reads=[bres[bk], R_modT, R_x[c]], writes=[R_x[c]])
            else:
                t = tmp[:, c % 2, 0:T]
                kb.op("dve", lambda: nc.vector.tensor_tensor(out=t, in0=banks[bk][:, 0:T], in1=smod[:, 2, c, :], op=ALU.mult),
                      reads=[bres[bk], R_smod], writes=[R_tmp[c % 2]])
                kb.op("dve", lambda: nc.vector.tensor_tensor(out=xs, in0=xs, in1=t, op=ALU.add),
                      reads=[R_tmp[c % 2], R_x[c]], writes=[R_x[c]])

        def gmlp_consts(v):
            (R_w32, R_bsb) = new_phase(["w32", "bsb"])
            w32 = carve("w32", 0, [128, NG * 128], F32)
            bsb = carve("bsb", 4096, [128, NG * 128], F32)
            if True:
                kb.dma("sp", w32[:], wst_d[v], kb.dsem("gc_w"), writes=[R_w32])
                kb.dma("sp", bsb[:], bs_d[v:v + 1, :].broadcast_to([128, NG * 128]), kb.dsem("gc_b"), writes=[R_bsb])
                mcol = C_LT if v == 0 else C_LTB
                kb.op("dve", lambda v=v, mcol=mcol: nc.vector.tensor_tensor(
                    out=w32[:].rearrange("p (g t) -> p g t", t=128), in0=w32[:].rearrange("p (g t) -> p g t", t=128),
                    in1=cst[:, mcol:mcol + 128].unsqueeze(1).broadcast_to([128, NG, 128]), op=ALU.mult),
                    reads=[R_w32, R_cst], writes=[R_w32])
                kb.op("act", lambda v=v: nc.scalar.copy(out=wstb[:, :], in_=w32[:]), reads=[R_w32], writes=[R_wstb])
                bks = [auxb.next(), auxb.next()]
                for hh in range(2):
                    kb.op("pe", lambda hh=hh, bks=bks: nc.tensor.matmul(
                        banks[bks[hh]][:, :], lhsT=cst[:, C_ONES:C_ONES + 128], rhs=w32[:, hh * 512:(hh + 1) * 512],
                        start=True, stop=True), reads=[R_w32, R_cst], writes=[bres[bks[hh]]])
                for j in range(16):
                    g = j // 2
                    bk = bks[g // 4]
                    kb.op("dve", lambda v=v, j=j, g=g, bk=bk: nc.vector.scalar_tensor_tensor(
                        out=Rt[:, j, :], in0=banks[bk][:, (g % 4) * 128:(g % 4 + 1) * 128],
                        scalar=pfm[:, PF_LNB + j:PF_LNB + j + 1], op0=ALU.mult,
                        in1=bsb[:, g * 128:(g + 1) * 128], op1=ALU.add),
                        reads=[bres[bk], R_pfm, R_bsb], writes=[R_Rt])

        def gmlp(tiles, l, sample):
            nt = len(tiles)
            T = nt * 128
            var = 1 if sample else 0
            names = ["z0", "z1", "n0", "n1", "n2", "n3", "u0", "u1", "us", "lng", "lnb", "ssb0", "ssb1", "st"]
            rr = new_phase(names)
            R_z = rr[0:2]; R_n = rr[2:6]; R_u = rr[6:8]; R_us = rr[8]; R_lng = rr[9]; R_lnb = rr[10]; R_ssb = rr[11:13]; R_st = rr[13]
            z_tm = carve("z_tm", 0, [128, 2, DGM], F32)
            n_tm = carve("n_tm", 16384, [128, 4, DGM], BF16)
            u_tmp = carve("u_tmp", 32768, [128, 2, 512], F32)
            us_fm = carve("us_fm", 36864, [128, 16, 512], BF16)
            lng_b = carve("lng_b", 53248, [128, DGM], F32)
            lnb_b = carve("lnb_b", 61440, [128, DGM], F32)
            ssb = carve("ssb", 69632, [128, 2, 128 * 2], F32)
            st = kb.sb("gm_stats", [128, 16], F32)
            bst = kb.sb("gm_bst", [128, 24], F32)
            R_us16 = [Res("us%d" % j) for j in range(16)]
            kb.fence_all([R_us], R_us16)
            vslots = [ws.get("sc", SL_GV + nb, hold=(nb > 0)) for nb in range(4)]
            for ti, tg in enumerate(tiles):
                zb = ti % 2
                need_v = sample or tg == 15
                if need_v and ti == nt - 1:
                    kb.dma("sp", lng_b[:], lngb_d[0:1, :].broadcast_to([128, DGM]), kb.dsem("lng"), writes=[R_lng])
                    kb.dma("sp", lnb_b[:], lngb_d[1:2, :].broadcast_to([128, DGM]), kb.dsem("lnb"), writes=[R_lnb])
                for nb in range(4):
                    slot, sr = vslots[nb]
                    sv = slot[:].rearrange("p (k c) -> p k c", c=512)
                    bk = mmb.next()
                    for kc in range(KC):
                        kb.op("pe", lambda kc=kc, sv=sv, bk=bk, ti=ti: nc.tensor.matmul(
                            banks[bk][:, :], lhsT=h_fm[:, kc, ti * 128:(ti + 1) * 128], rhs=sv[:, kc, :],
                            start=(kc == 0), stop=(kc == KC - 1)), reads=[sr, R_h[kc]], writes=[bres[bk]], lhs=[R_h[kc]])
                    kb.op("act", lambda nb=nb, bk=bk, zb=zb: nc.scalar.activation(
                        out=z_tm[:, zb, nb * 512:(nb + 1) * 512], in_=banks[bk][:, :], func=AF.Gelu_apprx_tanh),
                        reads=[bres[bk]], writes=[R_z[zb]])
                    kb.op("dve", lambda nb=nb, zb=zb: nc.vector.bn_stats(out=bst[:, nb * 6:(nb + 1) * 6],
                                                                         in_=z_tm[:, zb, nb * 512:(nb + 1) * 512]),
                          reads=[R_z[zb]], writes=[R_st])
                kb.op("dve", lambda: nc.vector.bn_aggr(out=st[:, 6:8], in_=bst[:, 0:24]), reads=[R_st], writes=[R_st])
                kb.op("dve", lambda: nc.vector.tensor_scalar(out=st[:, 8:9], in0=st[:, 7:8], scalar1=EPS, scalar2=None, op0=ALU.add),
                      reads=[R_st], writes=[R_st])
                kb.op("pool", lambda: nc.gpsimd.tensor_tensor(out=st[:, 10:11], in0=st[:, 8:9], in1=mhalf[:, 0:1], op=ALU.pow),
                      reads=[R_st, R_ones], writes=[R_st])
                kb.op("dve", lambda zb=zb, ti=ti: nc.vector.tensor_scalar(
                    out=n_tm[:, ti, :], in0=z_tm[:, zb, :], scalar1=st[:, 6:7], scalar2=st[:, 10:11],
                    op0=ALU.subtract, op1=ALU.mult), reads=[R_z[zb], R_st], writes=[R_n[ti]])
                if need_v:
                    kb.op("dve", lambda: nc.vector.scalar_tensor_tensor(out=st[:, 11:12], in0=st[:, 6:7], scalar=-1.0,
                                                                        op0=ALU.mult, in1=st[:, 10:11], op1=ALU.mult),
                          reads=[R_st], writes=[R_st])
                if ti == 0:
                    dump("n_tm", n_tm[:, 0, :], [R_n[0]], [128, DGM], BF16)
                    dump("z_tm", z_tm[:, 0, :], [R_z[0]], [128, DGM])
                if need_v:
                    kb.op("act", lambda zb=zb: nc.scalar.activation(
                        out=z_tm[:, zb, :], in_=z_tm[:, zb, :], func=AF.Identity, bias=st[:, 11:12], scale=st[:, 10:11]),
                        reads=[R_z[zb], R_st], writes=[R_z[zb]])
                    kb.op("dve", lambda zb=zb: nc.vector.tensor_tensor(out=z_tm[:, zb, :], in0=z_tm[:, zb, :], in1=lng_b[:],
                                                                       op=ALU.mult), reads=[R_z[zb], R_lng], writes=[R_z[zb]])
                    kb.op("dve", lambda zb=zb: nc.vector.tensor_tensor(out=z_tm[:, zb, :], in0=z_tm[:, zb, :], in1=lnb_b[:],
                                                                       op=ALU.add), reads=[R_z[zb], R_lnb], writes=[R_z[zb]])
                    kb.dma("pool", (gmv_s if sample else gmv_p)[:, :], z_tm[:, zb, :], kb.dsem("st_z%d" % zb), reads=[R_z[zb]], is_out=True)
            for su in range(4):
                slot, sr = ws.get("sc", SL_GU + su)
                sv = slot[:].rearrange("p (m k c) -> p m k c", m=4, k=KC)
                for mi in range(4):
                    j = su * 4 + mi
                    g = j // 2
                    ub = j % 2
                    bk = mmb.next()
                    for kc in range(KC):
                        kb.op("pe", lambda kc=kc, sv=sv, mi=mi, bk=bk: nc.tensor.matmul(
                            banks[bk][:, 0:T], lhsT=sv[:, mi, kc, :], rhs=h_fm[:, kc, 0:T],
                            start=(kc == 0), stop=(kc == KC - 1)), reads=[sr, R_h[kc]], writes=[bres[bk]], lhs=[sr])
                    kb.op("act", lambda bk=bk, ub=ub: nc.scalar.activation(out=u_tmp[:, ub, 0:T], in_=banks[bk][:, 0:T],
                                                                           func=AF.Gelu_apprx_tanh),
                          reads=[bres[bk]], writes=[R_u[ub]])
                    bk2 = auxb.next()
                    for ti in range(nt):
                        kb.op("pe", lambda ti=ti, j=j, g=g, bk2=bk2: nc.tensor.matmul(
                            banks[bk2][:, ti * 128:(ti + 1) * 128], lhsT=n_tm[:, ti, j * 128:(j + 1) * 128],
                            rhs=wstb[:, g * 128:(g + 1) * 128], start=True, stop=True),
                            reads=[R_n[ti], R_wstb], writes=[bres[bk2]], lhs=[R_n[ti]])
                    sb_ = ssb[:, ub, :].rearrange("p (a b) -> p a b", b=128) if False else None
                    s_t = kb.sb("gm_s%d" % ub, [128, 512], F32)
                    kb.op("dve", lambda j=j, bk2=bk2, s_t=s_t: nc.vector.scalar_tensor_tensor(
                        out=s_t[:, 0:T].rearrange("p (a b) -> p a b", b=128),
                        in0=banks[bk2][:, 0:T].rearrange("p (a b) -> p a b", b=128),
                        scalar=pfm[:, PF_LNG + j:PF_LNG + j + 1], op0=ALU.mult,
                        in1=Rt[:, j, :].unsqueeze(1).broadcast_to([128, nt, 128]), op1=ALU.add),
                        reads=[bres[bk2], R_pfm, R_Rt], writes=[R_ssb[ub]])
                    kb.op("pool", lambda j=j, ub=ub, s_t=s_t: nc.gpsimd.tensor_tensor(
                        out=us_fm[:, j, 0:T], in0=u_tmp[:, ub, 0:T], in1=s_t[:, 0:T], op=ALU.mult),
                        reads=[R_u[ub], R_ssb[ub]], writes=[R_us16[j]])
            tmpR = [Res("gtmp0"), Res("gtmp1")]
            kb.fence_all(R_z, tmpR)
            tmp = z_tm[:, :, 0:512]
            for so in range(4):
                slot, sr = ws.get("sc", SL_GO + so)
                sv = slot[:].rearrange("p (m k c) -> p m k c", m=2, k=16)
                for mi in range(2):
                    c = so * 2 + mi
                    bk = mmb.next()
                    for kc in range(16):
                        kb.op("pe", lambda kc=kc, sv=sv, mi=mi, bk=bk: nc.tensor.matmul(
                            banks[bk][:, 0:T], lhsT=sv[:, mi, kc, :], rhs=us_fm[:, kc, 0:T],
                            start=(kc == 0), stop=(kc == 15)), reads=[sr, R_us16[kc]], writes=[bres[bk]], lhs=[sr])
                    resid_add(T, c, bk, l, 0, sample, tmp, tmpR)
            arena_live.extend(R_us16 + tmpR)

        def mlp(T, l, sample):
            rr = new_phase(["sqa", "sqb", "nt0", "nt1"] + ["nsq%d" % k for k in range(KC)] + ["hid%d" % j for j in range(32)])
            R_sqt = rr[0:2]; R_tmp = rr[2:4]; R_sq = rr[4:12]; R_hid = rr[12:44]
            hid = carve("hid", 0, [128, 32, 512], BF16)
            sqt = carve("sqt", 32768, [128, 2, 512], F32)
            tmp = carve("ntmp", 36864, [128, 2, 512], F32)
            sq_buf = carve("nsq", 40960, [128, KC, 512], BF16)
            norm_mod(T, l, 1, sample, sq_buf, R_sq, tmp, R_tmp)
            s1 = SL_M1_0 if l == 0 else SL_M1_1
            s2 = SL_M2_0 if l == 0 else SL_M2_1
            for s in range(8):
                slot, sr = ws.get("sc", s1 + s)
                sv = slot[:].rearrange("p (m k c) -> p m k c", m=4, k=KC)
                for mi in range(4):
                    j = s * 4 + mi
                    bk = mmb.next()
                    for kc in range(KC):
                        kb.op("pe", lambda kc=kc, sv=sv, mi=mi, bk=bk: nc.tensor.matmul(
                            banks[bk][:, 0:T], lhsT=sv[:, mi, kc, :], rhs=h_fm[:, kc, 0:T],
                            start=(kc == 0), stop=(kc == KC - 1)), reads=[sr, R_h[kc]], writes=[bres[bk]], lhs=[sr])
                    qb = j % 2
                    kb.op("act", lambda bk=bk, qb=qb: nc.scalar.activation(out=sqt[:, qb, 0:T], in_=banks[bk][:, 0:T],
                                                                           func=AF.Square),
                          reads=[bres[bk]], writes=[R_sqt[qb]])
                    kb.op("dve", lambda bk=bk, qb=qb, j=j: nc.vector.scalar_tensor_tensor(
                        out=hid[:, j, 0:T], in0=banks[bk][:, 0:T], scalar=0.0, op0=ALU.is_gt, in1=sqt[:, qb, 0:T],
                        op1=ALU.mult), reads=[bres[bk], R_sqt[qb]], writes=[R_hid[j]])
            if l == 0 and T == 512:
                dump("hid", hid[:, :, :], R_hid, [128, 32, 512], BF16)
                dump("h2", h_fm[:, :, :], R_h, [128, KC, 512], BF16)
            for c in range(8):
                slot, sr = ws.get("sc", s2 + c)
                sv = slot[:].rearrange("p (k c) -> p k c", c=128)
                bk = mmb.next()
                for kc in range(32):
                    kb.op("pe", lambda kc=kc, sv=sv, bk=bk: nc.tensor.matmul(
                        banks[bk][:, 0:T], lhsT=sv[:, kc, :], rhs=hid[:, kc, 0:T], start=(kc == 0), stop=(kc == 31)),
                        reads=[sr, R_hid[kc]], writes=[bres[bk]], lhs=[sr])
                resid_add(T, c, bk, l, 1, sample, tmp, R_tmp)


        R_ST, R_STb, R_halo, R_cvst, R_mc = kb.res("ST"), kb.res("STb"), kb.res("halo"), kb.res("cvst"), kb.res("mconst")

        def mamba_consts():
            (R_w,) = new_phase(["wdt32"])
            w32 = carve("wdt32", 0, [128, KC * NH], F32)
            kb.dma("sp", w32[:], wdt_d[:, :], kb.dsem("wdt"), writes=[R_w])
            kb.op("dve", lambda: nc.vector.tensor_copy(out=wdt_bf[:].rearrange("p k h -> p (k h)"), in_=w32[:]),
                  reads=[R_w], writes=[R_mc])
            kb.op("dve", lambda: nc.vector.tensor_copy(out=identb[:], in_=ident), reads=[R_cst], writes=[R_mc])
            kb.op("dve", lambda: nc.vector.tensor_copy(out=negib[:], in_=cst[:, C_NEGI:C_NEGI + 128]), reads=[R_cst], writes=[R_mc])
            kb.op("act", lambda: nc.scalar.activation(out=acol[:], in_=p32[:, 1:2], func=AF.Exp), reads=[R_p32], writes=[R_mc])
            kb.op("dve", lambda: nc.vector.tensor_scalar(out=acol[:], in0=acol[:], scalar1=-1.0, scalar2=None, op0=ALU.mult),
                  reads=[R_mc], writes=[R_mc])
            kb.op("dve", lambda: nc.vector.memset(ST, 0.0), writes=[R_ST])
            kb.op("dve", lambda: nc.vector.memset(STb, 0.0), writes=[R_STb])
            kb.op("dve", lambda: nc.vector.memset(halo[:], 0.0), writes=[R_halo])

        def mamba_masks(sample):
            kc_ = C_KILLB if sample else C_KILL
            kb.op("dve", lambda: nc.vector.tensor_copy(
                out=killb[:], in_=cst[:, kc_:kc_ + 128].unsqueeze(1).broadcast_to([128, 4, 128])),
                reads=[R_cst], writes=[R_mc])

        def mamba(tiles, c0, sample, last_prompt):
            nt = len(tiles)
            T = nt * 128
            TP = 256 if not sample else 128
            l = 1
            LTm = cst[:, (C_LTB if sample else C_LT):(C_LTB if sample else C_LT) + 128]
            Um = cst[:, (C_UB if sample else C_U):(C_UB if sample else C_U) + 128]
            ONm = cst[:, (C_SAME if sample else C_ONES):(C_SAME if sample else C_ONES) + 128]
            names = (["zs%d" % i for i in range(16)] + ["xc%d" % i for i in range(16)] + ["bc%d" % i for i in range(16)]
                     + ["yg%d" % i for i in range(16)]
                     + ["xdt", "xdtd", "btm", "rda", "E", "M", "ytm", "t1", "xpre0", "xpre1", "dg0", "dg1", "small", "yf", "yz", "ysq",
                        "ntmp0", "ntmp1", "dtda"])
            rr = new_phase(names)
            R_zs = rr[0:16]; R_xc = rr[16:32]; R_bc = rr[32:48]; R_yg = rr[48:64]
            (R_xdt, R_xdtd, R_btm, _r1, _r2, _r3, _r4, R_t1, R_xp0, R_xp1, R_dg0, R_dg1, R_small, _r5, _r6, _r7,
             R_nt0, R_nt1, R_dtda) = rr[64:]
            R_xp = [R_xp0, R_xp1]; R_dgp = [R_dg0, R_dg1]
            R_dgv = [Res("dgv0"), Res("dgv1")]
            for r_ in R_dgv:
                r_.r = dict(rr[0].r)
            arena_live.extend(R_dgv)
            o = 0
            def cv(name, shape, dt):
                nonlocal o
                nb = int(np.prod(shape[1:])) * (4 if dt == F32 else 2)
                nb = (nb + 31) // 32 * 32
                v = carve(name, o, shape, dt)
                o += nb
                return v
            zs = cv("zs", [128, 16, TP], BF16)
            xc = cv("xc", [128, 16, TP], F32)
            BC = cv("BC", [128, 16, TP], BF16)
            yg = cv("yg", [128, 16, TP], BF16)
            xdt = cv("xdt", [128, 2048], BF16)
            xdtd = cv("xdtd", [128, 2048], BF16)
            B_tm = cv("B_tm", [128, 1024], BF16)
            NB = 1 if sample else 2
            rda = cv("rda", [128, NB, 4, 128], F32)
            E = cv("E", [128, NB, 512], F32)
            Mh = cv("Mh", [128, 2 * NB, 512], BF16)
            y_tm = cv("y_tm", [128, NB, 512], F32)
            t1 = cv("t1", [128, 512], F32) if not sample else None
            xpre = cv("xpre", [128, 2, TP + 8], BF16)
            dg = cv("dg", [128, 2, 4, 128], BF16)
            small = cv("small", [128, 8, 32], F32)
            yf = cv("yf", [128, NB, 4, 128], F32)
            yz = cv("yz", [128, NB, 4, 128], F32)
            ysq = cv("ysq", [128, NB, 4, 128], BF16)
            ntmp = cv("ntmp", [128, 2, TP], F32) if sample else None
            R_rda = [Res("rda%d" % i) for i in range(NB)]
            R_E = [Res("E%d" % i) for i in range(NB)]
            R_M = [Res("M%d" % i) for i in range(2 * NB)]
            R_ytm = [Res("ytm%d" % i) for i in range(NB)]
            R_yf = [Res("yf%d" % i) for i in range(NB)]
            R_yz = [Res("yz%d" % i) for i in range(NB)]
            R_ysq = [Res("ysq%d" % i) for i in range(NB)]
            extra = R_rda + R_E + R_M + R_ytm + R_yf + R_yz + R_ysq
            for r_ in extra:
                r_.r = dict(rr[0].r)
            arena_live.extend(extra)
            if sample:
                S0bf = cv("S0bf", [128, 2048], BF16)
                S0T = cv("S0T", [128, 2048], BF16)
                t1_all = cv("t1_all", [128, 2048], F32)
                seqmask = cv("seqmask", [128, NSAMP, 128], BF16)
                Cm = cv("Cm", [128, NG, 128], BF16)
                Bm = cv("Bm", [128, 1024], BF16)
                cvs6 = cv("cvs6", [128, 32, 48], F32)
                cd_fm = cv("cd_fm", [128, 256], F32)
                halo_s = cv("halo_s", [128, 32, 48], BF16)
                xpre_s = cv("xpre_s", [128, 2, NSAMP, 11], BF16)
                cvs6_off = o - 0
                (R_S0bf, R_S0T, R_t1all, R_seqm, R_Cm, R_Bm, R_cvs6, R_cdfm, R_halos, R_scv, R_S0a, R_S0b) = [
                    Res(n) for n in ("S0bf", "S0T", "t1all", "seqm", "Cm", "Bm", "cvs6", "cdfm", "halos", "scv", "S0a", "S0b")]
                fresh = [R_S0bf, R_S0T, R_t1all, R_seqm, R_Cm, R_Bm, R_cvs6, R_cdfm, R_halos, R_scv]
                for r_ in fresh:
                    r_.r = dict(rr[0].r)
                arena_live.extend(fresh + [R_S0a, R_S0b])
                kb.fence_all([R_Rt], [R_S0a])
                kb.fence_all(R_x, [R_S0b])
                S0buf = [Rt[:, :, :].rearrange("p (k two) n -> p k two n", two=2),
                         x_fm[:, :, 128:384].rearrange("p k (two n) -> p k two n", two=2)]
                R_S0 = [R_S0a, R_S0b]
                stg = carve("scvstg", 0, [128, CONVD], F32)
                kb.dma("sp", stg[0:48, :], scv[:, :], kb.dsem("scv"), writes=[R_scv])
                for r in range(4):
                    bk = auxb.next()
                    for i in range(8):
                        j = r * 8 + i
                        kb.op("pe", lambda j=j, i=i, bk=bk: nc.tensor.transpose(out=banks[bk][:, i * 48:(i + 1) * 48],
                                                                                in_=stg[0:48, j * 128:(j + 1) * 128],
                                                                                identity=ident[0:48, 0:48]),
                              reads=[R_scv, R_cst], writes=[bres[bk]])
                    kb.op("dve", lambda r=r, bk=bk: nc.vector.tensor_copy(
                        out=halo_s[:, r * 8:(r + 1) * 8, :].rearrange("p a b -> p (a b)"), in_=banks[bk][:, 0:384]),
                        reads=[bres[bk]], writes=[R_halos])
                kb.fence_all([R_scv], R_zs + R_xc + R_bc)
            dt_tm = small[:, 0, :]; da_tm = small[:, 1, :]; acum_sb = small[:, 2, :]; ea = small[:, 3, :]
            dte = small[:, 4, :]; w2 = small[:, 5, :]; cdv = small[:, 6, :]; dtmp = small[:, 7, :]
            cs = slice(c0, c0 + T)

            for s_ in range(4):
                slot, sr = ws.get("sc", SL_SI + s_)
                sv = slot[:].rearrange("p (m k c) -> p m k c", m=4, k=KC)
                for mi in range(4):
                    m = s_ * 4 + mi
                    bk = mmb.next()
                    for kc in range(KC):
                        kb.op("pe", lambda kc=kc, sv=sv, mi=mi, bk=bk: nc.tensor.matmul(
                            banks[bk][:, 0:T], lhsT=sv[:, mi, kc, :], rhs=h_fm[:, kc, cs],
                            start=(kc == 0), stop=(kc == KC - 1)), reads=[sr, R_h[kc]], writes=[bres[bk]], lhs=[sr])
                    kb.op("act", lambda bk=bk, m=m: nc.scalar.activation(out=zs[:, m, 0:T], in_=banks[bk][:, 0:T], func=AF.Silu),
                          reads=[bres[bk]], writes=[R_zs[m]])
            pending = None
            for s_ in range(8):
                slot, sr = ws.get("sc", SL_SI + 4 + s_)
                sv = slot[:].rearrange("p (m k c) -> p m k c", m=4, k=KC)
                for mi in range(4):
                    j = s_ * 4 + mi
                    jb = j % 2
                    bk = mmb.next()
                    for kc in range(KC):
                        kb.op("pe", lambda kc=kc, sv=sv, mi=mi, bk=bk: nc.tensor.matmul(
                            banks[bk][:, 0:T], lhsT=sv[:, mi, kc, :], rhs=h_fm[:, kc, cs],
                            start=(kc == 0), stop=(kc == KC - 1)), reads=[sr, R_h[kc]], writes=[bres[bk]], lhs=[sr])
                    for k in range(4):
                        if k < 2:
                            kb.op("pool", lambda k=k, j=j, jb=jb: nc.gpsimd.tensor_scalar(
                                out=dg[:, jb, k, :], in0=identb[:], scalar1=pfm[:, PF_CW + k * 32 + j:PF_CW + k * 32 + j + 1],
                                scalar2=1.0, op0=ALU.mult, op1=ALU.mult), reads=[R_mc, R_pfm], writes=[R_dgp[jb]])
                        else:
                            kb.op("dve", lambda k=k, j=j, jb=jb: nc.vector.tensor_scalar(
                                out=dg[:, jb, k, :], in0=identb[:], scalar1=pfm[:, PF_CW + k * 32 + j:PF_CW + k * 32 + j + 1],
                                scalar2=None, op0=ALU.mult), reads=[R_mc, R_pfm], writes=[R_dgv[jb]])
                    if sample:
                        kb.op("dve", lambda j=j, jb=jb: nc.vector.tensor_copy(
                            out=xpre_s[:, jb, :, 0:3], in_=halo_s[:, j, :].rearrange("p (b k) -> p b k", k=3)),
                            reads=[R_halos], writes=[R_xp[jb]])
                        kb.op("dve", lambda bk=bk, jb=jb: nc.vector.tensor_copy(
                            out=xpre_s[:, jb, :, 3:11], in_=banks[bk][:, 0:128].rearrange("p (b t) -> p b t", t=LS)),
                            reads=[bres[bk]], writes=[R_xp[jb]])
                        kb.op("dve", lambda bk=bk, j=j: nc.vector.tensor_copy(
                            out=cvs6[:, j, :].rearrange("p (b k) -> p b k", k=3),
                            in_=banks[bk][:, 0:128].rearrange("p (b t) -> p b t", t=LS)[:, :, 5:8]),
                            reads=[bres[bk]], writes=[R_cvs6])
                    else:
                        kb.op("dve", lambda j=j, jb=jb: nc.vector.tensor_copy(out=xpre[:, jb, 0:3], in_=halo[:, j, :]),
                              reads=[R_halo], writes=[R_xp[jb]])
                        kb.op("dve", lambda bk=bk, jb=jb: nc.vector.tensor_copy(out=xpre[:, jb, 3:3 + T], in_=banks[bk][:, 0:T]),
                              reads=[bres[bk]], writes=[R_xp[jb]])
                        kb.op("dve", lambda j=j, jb=jb: nc.vector.tensor_copy(out=halo[:, j, :], in_=xpre[:, jb, T:T + 3]),
                              reads=[R_xp[jb]], writes=[R_halo])
                    if last_prompt:
                        kb.op("dve", lambda j=j, bk=bk: nc.vector.tensor_copy(out=cvst[:, j, :], in_=banks[bk][:, T - 3:T]),
                              reads=[bres[bk]], writes=[R_cvst])
                    def conv_emit(j=j, jb=jb):
                        b2 = auxb.next()
                        for k in range(4):
                            kb.op("pe", lambda k=k, jb=jb, b2=b2: nc.tensor.matmul(
                                banks[b2][:, 0:T], lhsT=dg[:, jb, k, :],
                                rhs=(xpre_s[:, jb, :, k:k + LS] if sample else xpre[:, jb, k:k + T]), start=(k == 0), stop=(k == 3)),
                                reads=[R_dgp[jb], R_dgv[jb], R_xp[jb]], writes=[bres[b2]], lhs=[R_dgp[jb], R_dgv[jb]])
                        if j < 16:
                            kb.op("act", lambda j=j, b2=b2: nc.scalar.activation(
                                out=xc[:, j, 0:T], in_=banks[b2][:, 0:T], func=AF.Silu, bias=pfm[:, PF_CB + j:PF_CB + j + 1], scale=1.0),
                                reads=[bres[b2], R_pfm], writes=[R_xc[j]])
                        else:
                            kb.op("act", lambda j=j, b2=b2: nc.scalar.activation(
                                out=BC[:, j - 16, 0:T], in_=banks[b2][:, 0:T], func=AF.Silu, bias=pfm[:, PF_CB + j:PF_CB + j + 1], scale=1.0),
                                reads=[bres[b2], R_pfm], writes=[R_bc[j - 16]])
                    if pending is not None:
                        pending()
                    pending = conv_emit
            pending()
            bk = mmb.next()
            for kc in range(KC):
                kb.op("pe", lambda kc=kc, bk=bk: nc.tensor.matmul(
                    banks[bk][0:32, 0:T], lhsT=wdt_bf[:, kc, :], rhs=h_fm[:, kc, cs], start=(kc == 0), stop=(kc == KC - 1)),
                    reads=[R_mc, R_h[kc]], writes=[bres[bk]])
            kb.op("act", lambda bk=bk: nc.scalar.activation(out=dtda_fm[:, 0, 0:T], in_=banks[bk][0:32, 0:T], func=AF.Softplus,
                                                            bias=p32[:, 0:1], scale=1.0),
                  reads=[bres[bk], R_p32], writes=[R_dtda])
            kb.op("dve", lambda: nc.vector.tensor_scalar(out=dtda_fm[:, 1, 0:T], in0=dtda_fm[:, 0, 0:T], scalar1=acol[:, 0:1],
                                                         scalar2=None, op0=ALU.mult), reads=[R_dtda, R_mc], writes=[R_dtda])

            early_ap = []
            for ci in range(nt):
                cc = slice(ci * 128, (ci + 1) * 128)
                for i in range(2):
                    kb.op("pe", lambda i=i: nc.tensor.transpose(out=banks[6][:, i * 32:(i + 1) * 32], in_=dtda_fm[:, i, cc],
                                                                identity=ident[0:32, 0:32]),
                          reads=[R_dtda, R_cst], writes=[bres[6]])
                kb.op("dve", lambda: nc.vector.tensor_copy(out=small[:, 0:2, :].rearrange("p a h -> p (a h)"), in_=banks[6][:, 0:64]),
                      reads=[bres[6]], writes=[R_small])
                for g_ in range(2):
                    kb.op("pool", lambda g_=g_: nc.gpsimd.tensor_tensor(
                        out=rda[:, g_ % NB, :, :], in0=LTm.unsqueeze(1).broadcast_to([128, 4, 128]),
                        in1=da_tm[:, g_ * 4:(g_ + 1) * 4].unsqueeze(2).broadcast_to([128, 4, 128]), op=ALU.mult),
                        reads=[R_cst, R_small], writes=[R_rda[g_ % NB]]) if NB == 2 else None
                kb.op("pe", lambda: nc.tensor.matmul(banks[7][:, 0:32], lhsT=LTm, rhs=da_tm, start=True, stop=True),
                      reads=[R_cst, R_small], writes=[bres[7]])
                kb.op("pe", lambda: nc.tensor.matmul(banks[7][:, 32:64], lhsT=ONm, rhs=da_tm, start=True, stop=True),
                      reads=[R_cst, R_small], writes=[bres[7]])
                kb.op("act", lambda: nc.scalar.copy(out=acum_sb, in_=banks[7][:, 0:32]), reads=[bres[7]], writes=[R_small])
                kb.op("act", lambda: nc.scalar.activation(out=ea, in_=banks[7][:, 0:32], func=AF.Exp), reads=[bres[7]], writes=[R_small])
                kb.op("act", lambda: nc.scalar.activation(out=cdv, in_=banks[7][:, 32:64], func=AF.Exp), reads=[bres[7]], writes=[R_small])
                kb.op("dve", lambda: nc.vector.tensor_tensor(out=dtmp, in0=banks[7][:, 32:64], in1=acum_sb, op=ALU.subtract),
                      reads=[bres[7], R_small], writes=[R_small])
                kb.op("act", lambda: nc.scalar.activation(out=dte, in_=dtmp, func=AF.Exp), reads=[R_small], writes=[R_small])
                kb.op("dve", lambda: nc.vector.tensor_tensor(out=w2, in0=dt_tm, in1=dte, op=ALU.mult), reads=[R_small], writes=[R_small])
                for hp in range(16):
                    kb.op("pe", lambda hp=hp: nc.tensor.transpose(out=banks[hp // 4][:, (hp % 4) * 128:(hp % 4 + 1) * 128],
                                                                  in_=xc[:, hp, cc], identity=ident),
                          reads=[R_xc[hp], R_cst], writes=[bres[hp // 4]], lhs=[R_xc[hp]])
                for q in range(4):
                    kb.op("dve", lambda q=q: nc.vector.tensor_tensor(
                        out=xdt[:, q * 512:(q + 1) * 512].rearrange("p (h e) -> p h e", e=HD),
                        in0=banks[q][:, :].rearrange("p (h e) -> p h e", e=HD),
                        in1=dt_tm[:, q * 8:(q + 1) * 8].unsqueeze(2).broadcast_to([128, 8, HD]), op=ALU.mult),
                        reads=[bres[q], R_small], writes=[R_xdt])
                def stXD(_):
                    for hf in range(2):
                        kb.op("pool", lambda hf=hf: nc.gpsimd.tensor_tensor(
                            out=xdtd[:, hf * 1024:(hf + 1) * 1024].rearrange("p (h e) -> p h e", e=HD),
                            in0=xdt[:, hf * 1024:(hf + 1) * 1024].rearrange("p (h e) -> p h e", e=HD),
                            in1=dte[:, hf * 16:(hf + 1) * 16].unsqueeze(2).broadcast_to([128, 16, HD]), op=ALU.mult),
                            reads=[R_xdt, R_small], writes=[R_xdtd])
                if sample:
                    stXD(0)
                b6 = banks[6][:, :].bitcast(BF16)
                for g in range(NG):
                    kb.op("pe", lambda g=g: nc.tensor.transpose(out=b6[:, g * 128:(g + 1) * 128], in_=BC[:, g, cc], identity=identb[:]),
                          reads=[R_bc[g], R_mc], writes=[bres[6]], lhs=[R_bc[g]])
                kb.op("act", lambda: nc.scalar.copy(out=B_tm[:], in_=b6[:, 0:1024]), reads=[bres[6]], writes=[R_btm])
                if sample:
                    kb.dma("sp", t1_all[:], seqm_d[0:1, :].broadcast_to([128, NSAMP * 128]), kb.dsem("seqm"), writes=[R_t1all])
                    kb.op("dve", lambda: nc.vector.tensor_copy(out=seqmask[:].rearrange("p b t -> p (b t)"), in_=t1_all[:]),
                          reads=[R_t1all], writes=[R_seqm])
                    kb.op("dve", lambda: nc.vector.tensor_copy(
                        out=t1_all[:].rearrange("p (h e) -> p h e", e=HD), in_=da_tm.unsqueeze(2).broadcast_to([128, NH, HD])),
                        reads=[R_small], writes=[R_t1all])
                    for hp in range(16):
                        kb.op("pe", lambda hp=hp: nc.tensor.matmul(banks[6][:, hp * 16:(hp + 1) * 16],
                                                                   lhsT=t1_all[:, hp * 128:(hp + 1) * 128],
                                                                   rhs=cst[:, C_SEQIND:C_SEQIND + 16], start=True, stop=True),
                              reads=[R_t1all, R_cst], writes=[bres[6]])
                    kb.op("act", lambda: nc.scalar.activation(out=cd_fm[:], in_=banks[6][:, 0:256], func=AF.Exp),
                          reads=[bres[6]], writes=[R_cdfm])
                    b45 = [banks[4][:, :].bitcast(BF16), banks[5][:, :].bitcast(BF16)]
                    for b in range(NSAMP):
                        sb_ = b % 2
                        S0 = S0buf[sb_]
                        S0h = lambda hp, S0=S0: S0[:, hp // 2, hp % 2, :]
                        for two in range(2):
                            kb.dma("sp", S0[:, :, two, :], sst[b].rearrange("(k two h2) p n -> two (h2 p) k n", two=2, h2=2)[two],
                                   kb.dsem("s0in%d" % sb_), writes=[R_S0[sb_]])
                        for k2 in range(8):
                            kb.op("act", lambda k2=k2, S0=S0: nc.scalar.copy(
                                out=S0bf[:, k2 * 256:(k2 + 1) * 256].rearrange("p (two n) -> p two n", two=2), in_=S0[:, k2, :, :]),
                                reads=[R_S0[sb_]], writes=[R_S0bf])
                        for hp in range(16):
                            kb.op("pe", lambda hp=hp: nc.tensor.transpose(out=b45[hp // 8][:, (hp % 8) * 128:(hp % 8 + 1) * 128],
                                                                          in_=S0bf[:, hp * 128:(hp + 1) * 128], identity=identb[:]),
                                  reads=[R_S0bf, R_mc], writes=[bres[4 + hp // 8]])
                        for hf in range(2):
                            kb.op("dve", lambda hf=hf: nc.vector.tensor_copy(out=S0T[:, hf * 1024:(hf + 1) * 1024], in_=b45[hf][:, 0:1024]),
                                  reads=[bres[4 + hf]], writes=[R_S0T])
                        kb.op("dve", lambda b=b: nc.vector.tensor_tensor(
                            out=Cm[:], in0=BC[:, 8:16, cc], in1=seqmask[:, b, :].unsqueeze(1).broadcast_to([128, NG, 128]), op=ALU.mult),
                            reads=R_bc[8:16] + [R_seqm], writes=[R_Cm])
                        for g in range(NG):
                            kb.op("pe", lambda g=g, b=b: nc.tensor.matmul(
                                banks[g // 2][:, (g % 2) * 256:(g % 2 + 1) * 256], lhsT=Cm[:, g, :], rhs=S0T[:, g * 256:(g + 1) * 256],
                                start=(b == 0 and g % 2 == 0), stop=(b == NSAMP - 1 and g % 2 == 1)),
                                reads=[R_Cm, R_S0T], writes=[bres[g // 2]])
                        kb.op("dve", lambda b=b: nc.vector.tensor_scalar(out=Bm[:], in0=B_tm[:], scalar1=cst[:, C_SEQIND + b:C_SEQIND + b + 1],
                                                                         scalar2=None, op0=ALU.mult),
                              reads=[R_btm, R_cst], writes=[R_Bm])
                        for r4 in range(4):
                            bk = 6 + r4 % 2
                            for i in range(4):
                                hp = r4 * 4 + i
                                g = hp // 2
                                kb.op("pe", lambda hp=hp, i=i, g=g, bk=bk: nc.tensor.matmul(
                                    banks[bk][:, i * 128:(i + 1) * 128], lhsT=xdtd[:, hp * 128:(hp + 1) * 128],
                                    rhs=Bm[:, g * 128:(g + 1) * 128], start=True, stop=True),
                                    reads=[R_xdtd, R_Bm], writes=[bres[bk]])
                            for i in range(4):
                                hp = r4 * 4 + i
                                kb.op("dve", lambda hp=hp, i=i, bk=bk, b=b, S0h=S0h: nc.vector.scalar_tensor_tensor(
                                    out=S0h(hp), in0=S0h(hp), scalar=cd_fm[:, hp * 16 + b:hp * 16 + b + 1], op0=ALU.mult,
                                    in1=banks[bk][:, i * 128:(i + 1) * 128], op1=ALU.add),
                                    reads=[R_S0[sb_], R_cdfm, bres[bk]], writes=[R_S0[sb_]])
                        for two in range(2):
                            kb.dma("pool", ssm_s[b].rearrange("(k two q) n -> two q k n", two=2, q=128)[two], S0[:, :, two, :],
                                   kb.dsem("s0out%d" % sb_), reads=[R_S0[sb_]], is_out=True)
                    for q in range(4):
                        kb.op("dve", lambda q=q: nc.vector.tensor_tensor(
                            out=t1_all[:, q * 512:(q + 1) * 512].rearrange("p (h e) -> p h e", e=HD),
                            in0=banks[q][:, :].rearrange("p (h e) -> p h e", e=HD),
                            in1=ea[:, q * 8:(q + 1) * 8].unsqueeze(2).broadcast_to([128, 8, HD]), op=ALU.mult),
                            reads=[bres[q], R_small], writes=[R_t1all])
                for g in range(NG):
                    kb.op("pe", lambda g=g: nc.tensor.matmul(banks[4 + g // 4][:, (g % 4) * 128:(g % 4 + 1) * 128],
                                                             lhsT=BC[:, g, cc], rhs=BC[:, 8 + g, cc], start=True, stop=True),
                          reads=[R_bc[g], R_bc[8 + g]], writes=[bres[4 + g // 4]], lhs=[R_bc[g]])
                def stSTD(_):
                    kb.op("pool", lambda: nc.gpsimd.tensor_tensor(
                        out=ST.rearrange("p (h e) -> p h e", e=HD), in0=ST.rearrange("p (h e) -> p h e", e=HD),
                        in1=cdv.unsqueeze(2).broadcast_to([128, NH, HD]), op=ALU.mult),
                        reads=[R_ST, R_small], writes=[R_ST])

                def stAp(g):
                    rb = g % NB
                    kb.op("pool", lambda: nc.gpsimd.tensor_tensor(
                        out=rda[:, rb, :, :], in0=LTm.unsqueeze(1).broadcast_to([128, 4, 128]),
                        in1=da_tm[:, g * 4:(g + 1) * 4].unsqueeze(2).broadcast_to([128, 4, 128]), op=ALU.mult),
                        reads=[R_cst, R_small], writes=[R_rda[rb]])

                def stA(g):
                    rb = g % NB
                    mb = g % (2 * NB)
                    sbk = 6 + g % 2
                    kb.op("pe", lambda: nc.tensor.matmul(banks[sbk][:, :], lhsT=negib[:],
                                                         rhs=killb[:].rearrange("p a t -> p (a t)"), start=True, stop=False),
                          reads=[R_mc], writes=[bres[sbk]], lhs=[R_mc])
                    kb.op("pe", lambda: nc.tensor.matmul(banks[sbk][:, :], lhsT=Um, rhs=rda[:, rb, :, :].rearrange("p a t -> p (a t)"),
                                                         start=False, stop=True), reads=[R_cst, R_rda[rb]], writes=[bres[sbk]], lhs=[R_cst])
                    kb.op("act", lambda: nc.scalar.activation(out=E[:, rb, :], in_=banks[sbk][:, :], func=AF.Exp),
                          reads=[bres[sbk]], writes=[R_E[rb]])
                    cbk = 4 + g // 4
                    cb0 = (g % 4) * 128
                    kb.op("dve", lambda: nc.vector.tensor_tensor(
                        out=Mh[:, mb, :].rearrange("p (h t) -> p h t", h=4), in0=E[:, rb, :].rearrange("p (h t) -> p h t", h=4),
                        in1=banks[cbk][:, cb0:cb0 + 128].unsqueeze(1).broadcast_to([128, 4, 128]), op=ALU.mult),
                        reads=[R_E[rb], bres[cbk]], writes=[R_M[mb]])

                def stB1(q):
                    yb = q % NB
                    for h8 in range(8):
                        h = q * 8 + h8
                        mb = (2 * q + h8 // 4) % (2 * NB)
                        kb.op("pe", lambda h8=h8, h=h, mb=mb: nc.tensor.matmul(
                            banks[0][:, h8 * HD:(h8 + 1) * HD], lhsT=Mh[:, mb, (h8 % 4) * 128:(h8 % 4 + 1) * 128],
                            rhs=xdt[:, h * HD:(h + 1) * HD], start=True, stop=True),
                            reads=[R_M[mb], R_xdt], writes=[bres[0]], lhs=[R_M[mb]])
                    if sample:
                        kb.op("dve", lambda: nc.vector.tensor_tensor(out=y_tm[:, yb, :], in0=banks[0][:, :],
                                                                     in1=t1_all[:, q * 512:(q + 1) * 512], op=ALU.add),
                              reads=[bres[0], R_t1all], writes=[R_ytm[yb]])
                    else:
                        for gg in range(2):
                            g = 2 * q + gg
                            kb.op("pe", lambda gg=gg, g=g: nc.tensor.matmul(banks[1][:, gg * 256:(gg + 1) * 256], lhsT=BC[:, 8 + g, cc],
                                                                            rhs=STb[:, g * 256:(g + 1) * 256], start=True, stop=True),
                                  reads=[R_bc[8 + g], R_STb], writes=[bres[1]], lhs=[R_bc[8 + g]])
                        kb.op("dve", lambda: nc.vector.tensor_tensor(
                            out=t1[:].rearrange("p (h e) -> p h e", e=HD), in0=banks[1][:, :].rearrange("p (h e) -> p h e", e=HD),
                            in1=ea[:, q * 8:(q + 1) * 8].unsqueeze(2).broadcast_to([128, 8, HD]), op=ALU.mult),
                            reads=[bres[1], R_small], writes=[R_t1])
                        kb.op("dve", lambda: nc.vector.tensor_tensor(out=y_tm[:, yb, :], in0=banks[0][:, :], in1=t1[:], op=ALU.add),
                              reads=[bres[0], R_t1], writes=[R_ytm[yb]])

                def stB2(q):
                    yb = q % NB
                    for i in range(4):
                        kb.op("pe", lambda i=i: nc.tensor.transpose(out=banks[2][:, i * 128:(i + 1) * 128],
                                                                    in_=y_tm[:, yb, i * 128:(i + 1) * 128], identity=ident),
                              reads=[R_ytm[yb], R_cst], writes=[bres[2]], lhs=[R_ytm[yb]])
                    for i in range(4):
                        hp = 4 * q + i
                        kb.op("dve", lambda hp=hp, i=i: nc.vector.scalar_tensor_tensor(
                            out=yf[:, yb, i, :], in0=xc[:, hp, cc], scalar=pfm[:, PF_D + hp:PF_D + hp + 1], op0=ALU.mult,
                            in1=banks[2][:, i * 128:(i + 1) * 128], op1=ALU.add),
                            reads=[R_xc[hp], R_pfm, bres[2]], writes=[R_yf[yb]])
                    kb.op("pool", lambda: nc.gpsimd.tensor_tensor(out=yz[:, yb, :, :], in0=yf[:, yb, :, :], in1=zs[:, 4 * q:4 * q + 4, cc],
                                                                  op=ALU.mult),
                          reads=[R_yf[yb]] + R_zs[4 * q:4 * q + 4], writes=[R_yz[yb]])
                    kb.op("pool", lambda: nc.gpsimd.tensor_tensor(out=ysq[:, yb, :, :], in0=yz[:, yb, :, :], in1=yz[:, yb, :, :],
                                                                  op=ALU.mult), reads=[R_yz[yb]], writes=[R_ysq[yb]])

                def stB2b(q):
                    yb = q % NB
                    for gg in range(2):
                        for i2 in range(2):
                            kb.op("pe", lambda gg=gg, i2=i2: nc.tensor.matmul(banks[3][:, gg * 128:(gg + 1) * 128], lhsT=ones_g[:],
                                                                              rhs=ysq[:, yb, 2 * gg + i2, :], start=(i2 == 0), stop=(i2 == 1)),
                                  reads=[R_ysq[yb], R_ones], writes=[bres[3]], lhs=[R_ones])
                    kb.op("act", lambda: nc.scalar.activation(out=banks[3][:, 0:256], in_=banks[3][:, 0:256], func=AF.Ln,
                                                              bias=eps_t[:, 0:1], scale=1.0),
                          reads=[bres[3], R_ones], writes=[bres[3]])
                    kb.op("act", lambda: nc.scalar.activation(out=banks[3][:, 0:256], in_=banks[3][:, 0:256], func=AF.Exp, scale=-0.5),
                          reads=[bres[3]], writes=[bres[3]])
                    for i in range(4):
                        hp = 4 * q + i
                        kb.op("dve", lambda hp=hp, i=i: nc.vector.scalar_tensor_tensor(
                            out=yg[:, hp, cc], in0=yz[:, yb, i, :], scalar=pfm[:, PF_SNG + hp:PF_SNG + hp + 1], op0=ALU.mult,
                            in1=banks[3][:, (i // 2) * 128:(i // 2 + 1) * 128], op1=ALU.mult),
                            reads=[R_yz[yb], R_pfm, bres[3]], writes=[R_yg[hp]])

                if NB == 2:
                    order = [("A", 0), ("A", 1), ("Ap", 2), ("Ap", 3), ("A", 2), ("A", 3), ("B1", 0), ("Ap", 4), ("Ap", 5),
                             ("A", 4), ("A", 5), ("B2", 0), ("B1", 1), ("Ap", 6), ("Ap", 7), ("XD", 0), ("STD", 0), ("A", 6), ("A", 7),
                             ("B2b", 0),
                             ("B2", 1), ("B1", 2), ("B2b", 1), ("B2", 2), ("B1", 3), ("B2b", 2), ("B2", 3), ("B2b", 3)]
                else:
                    order = []
                    for q in range(4):
                        order += [("Ap", 2 * q), ("A", 2 * q), ("Ap", 2 * q + 1), ("A", 2 * q + 1), ("B1", q), ("B2", q), ("B2b", q)]
                for (st_, a_) in order:
                    {"Ap": stAp, "A": stA, "B1": stB1, "B2": stB2, "B2b": stB2b, "STD": stSTD, "XD": stXD}[st_](a_)
                if not sample:
                    for g in range(NG):
                        kb.op("pe", lambda g=g: nc.tensor.matmul(banks[g // 2][:, (g % 2) * 256:(g % 2 + 1) * 256],
                                                                 lhsT=B_tm[:, g * 128:(g + 1) * 128],
                                                                 rhs=xdtd[:, g * 256:(g + 1) * 256], start=True, stop=True),
                              reads=[R_btm, R_xdtd], writes=[bres[g // 2]], lhs=[R_btm])
                    for q in range(4):
                        kb.op("dve", lambda q=q: nc.vector.tensor_tensor(out=ST[:, q * 512:(q + 1) * 512], in0=banks[q][:, :],
                                                                         in1=ST[:, q * 512:(q + 1) * 512], op=ALU.add),
                              reads=[bres[q], R_ST], writes=[R_ST])
                    kb.op("act", lambda: nc.scalar.copy(out=STb, in_=ST), reads=[R_ST], writes=[R_STb])
            for so in range(4):
                slot, sr = ws.get("sc", SL_SO + so)
                sv = slot[:].rearrange("p (m k c) -> p m k c", m=2, k=16)
                for mi in range(2):
                    c = so * 2 + mi
                    bk = mmb.next()
                    for kc in range(16):
                        kb.op("pe", lambda kc=kc, sv=sv, mi=mi, bk=bk: nc.tensor.matmul(
                            banks[bk][:, 0:T], lhsT=sv[:, mi, kc, :], rhs=yg[:, kc, 0:T], start=(kc == 0), stop=(kc == 15)),
                            reads=[sr, R_yg[kc]], writes=[bres[bk]], lhs=[sr])
                    resid_add(T, c, bk, l, 0, sample, ntmp, [R_nt0, R_nt1], c0=c0)
            if sample:
                R_cvt = Res("cvt")
                kb.fence_all(R_zs + R_xc + R_bc + R_yg + [R_scv], [R_cvt])
                arena_live.append(R_cvt)
                cvt = carve("cvt", 0, [128, CONVD], F32)
                for r in range(8):
                    bk = auxb.next()
                    for i in range(4):
                        j = r * 4 + i
                        kb.op("pe", lambda j=j, i=i, bk=bk: nc.tensor.transpose(out=banks[bk][0:48, i * 128:(i + 1) * 128],
                                                                                in_=cvs6[:, j, :], identity=ident),
                              reads=[R_cvs6, R_cst], writes=[bres[bk]])
                    kb.op("dve", lambda r=r, bk=bk: nc.vector.tensor_copy(out=cvt[0:48, r * 512:(r + 1) * 512], in_=banks[bk][0:48, :]),
                          reads=[bres[bk]], writes=[R_cvt])
                kb.dma("pool", cv_s.rearrange("b k f -> (b k) f"), cvt[0:48, :], kb.dsem("st_cvt"), reads=[R_cvt], is_out=True)

        def mamba_prompt_out():
            rr = new_phase(["so0", "so1", "so2", "so3", "cvo"])
            so = carve("so", 0, [128, 16, 128], F32)
            cvo = carve("cvo", 8192, [128, CONVD], F32)
            for hp in range(16):
                kb.op("pe", lambda hp=hp: nc.tensor.transpose(out=banks[hp // 4][:, (hp % 4) * 128:(hp % 4 + 1) * 128],
                                                              in_=ST[:, hp * 128:(hp + 1) * 128], identity=ident),
                      reads=[R_ST, R_cst], writes=[bres[hp // 4]])
            for q in range(4):
                kb.op("act" if q % 2 else "dve", (lambda q=q: nc.scalar.copy(out=so[:, q * 4:(q + 1) * 4, :].rearrange("p a n -> p (a n)"), in_=banks[q][:, :])) if q % 2
                      else (lambda q=q: nc.vector.tensor_copy(out=so[:, q * 4:(q + 1) * 4, :].rearrange("p a n -> p (a n)"), in_=banks[q][:, :])),
                      reads=[bres[q]], writes=[rr[q]])
            kb.dma("pool", ssm_p.rearrange("(hp q) n -> q hp n", q=128), so[:, :, :], kb.dsem("st_so"), reads=rr[0:4], is_out=True)
            for r in range(8):
                bk = 4 + r % 4
                for i in range(4):
                    j = r * 4 + i
                    kb.op("pe", lambda j=j, i=i, bk=bk: nc.tensor.transpose(out=banks[bk][0:3, i * 128:(i + 1) * 128],
                                                                            in_=cvst[:, j, :], identity=ident),
                          reads=[R_cvst, R_cst], writes=[bres[bk]])
                kb.op("dve", lambda r=r, bk=bk: nc.vector.tensor_copy(out=cvo[0:3, r * 512:(r + 1) * 512], in_=banks[bk][0:3, :]),
                      reads=[bres[bk]], writes=[rr[4]])
            kb.dma("pool", cv_p[:, :], cvo[0:3, :], kb.dsem("st_cvo"), reads=[rr[4]], is_out=True)

        def final_out(tiles):
            T = len(tiles) * 128
            rr = new_phase(["ft0", "ft1", "yo0", "yo1"] + ["fsq%d" % k for k in range(KC)] + ["yf%d" % k for k in range(KC)])
            R_tmp = rr[0:2]; R_yo = rr[2:4]; R_sq = rr[4:12]; R_yf = rr[12:20]
            y_fm = carve("y_fm", 0, [128, KC, 512], F32)
            sq_buf = carve("fsq", 16384, [128, KC, 512], BF16)
            yo = carve("yo", 24576, [128, 2, D], F32)
            bk = rms_rstd(T, [x_fm[:, kc, 0:T] for kc in range(KC)], R_x, sq_buf, R_sq, ones_bf)
            for kc in range(KC):
                kb.op("dve", lambda kc=kc: nc.vector.scalar_tensor_tensor(
                    out=y_fm[:, kc, 0:T], in0=x_fm[:, kc, 0:T], scalar=pfm[:, PF_FG + kc:PF_FG + kc + 1], op0=ALU.mult,
                    in1=banks[bk][:, 0:T], op1=ALU.mult), reads=[R_x[kc], bres[bk], R_pfm], writes=[R_yf[kc]])
            for ti, tg in enumerate(tiles):
                ob = ti % 2
                for hh in range(2):
                    b2 = auxb.next()
                    for q in range(4):
                        kc = hh * 4 + q
                        kb.op("pe", lambda kc=kc, q=q, b2=b2, ti=ti: nc.tensor.transpose(
                            out=banks[b2][:, q * 128:(q + 1) * 128], in_=y_fm[:, kc, ti * 128:(ti + 1) * 128],
                            identity=ident), reads=[R_yf[kc], R_cst], writes=[bres[b2]], lhs=[R_yf[kc]])
                    e = evq.next()
                    dst = yo[:, ob, hh * 512:(hh + 1) * 512]
                    if e == "act":
                        kb.op("act", lambda dst=dst, b2=b2: nc.scalar.copy(out=dst, in_=banks[b2][:, :]),
                              reads=[bres[b2]], writes=[R_yo[ob]])
                    else:
                        kb.op("dve", lambda dst=dst, b2=b2: nc.vector.tensor_copy(out=dst, in_=banks[b2][:, :]),
                              reads=[bres[b2]], writes=[R_yo[ob]])
                kb.dma("pool", y_out[tg * 128:(tg + 1) * 128, :], yo[:, ob, :], kb.dsem("st_yo%d" % ob), reads=[R_yo[ob]], is_out=True)

        ada_layer(0)
        emit_cast([1, 2], [R_modT])
        ada_layer(1)
        ada1_done = True
        blocks = [[0, 1, 2, 3], [4, 5, 6, 7], [8, 9, 10, 11], [12, 13, 14, 15], [16]]
        for bi, tiles in enumerate(blocks):
            if bi in skip:
                continue
            sample = (bi == 4)
            T = len(tiles) * 128
            if sample:
                kb.fence_all([R_ST, R_STb], [R_smod])
            if bi == 0 or bi == 4:
                gmlp_consts(1 if sample else 0)
            rr = new_phase(["xt0", "xt1"])
            xt = carve("xt", 0, [128, 2, D], F32)
            load_x(tiles, rr, xt)
            rr = new_phase(["nt0", "nt1"] + ["nsq%d" % k for k in range(KC)])
            tmp = carve("ntmp", 0, [128, 2, 512], F32)
            sq_buf = carve("nsq", 4096, [128, KC, 512], BF16)
            if bi == 0:
                dump("x_fm", x_fm[:, :, :], R_x, [128, KC, 512])
            norm_mod(T, 0, 0, sample, sq_buf, rr[2:10], tmp, rr[0:2])
            if bi == 0:
                emit_cast([3, 4], [R_h[7]])
                dump("h_fm", h_fm[:, :, :], R_h, [128, KC, 512], BF16)
                dump("Rt", Rt[:, :, :], [R_Rt], [128, 16, 128])
                dump("wstb", wstb[:, :], [R_wstb], [128, 1024], BF16)
            gmlp(tiles, 0, sample)
            if bi == 0:
                emit_cast([5, 6], [R_x[7]])
                dump("x1", x_fm[:, :, :], R_x, [128, KC, 512])
            mlp(T, 0, sample)
            if bi == 0:
                emit_cast([7, 8], [R_x[7]])
                dump("x2", x_fm[:, :, :], R_x, [128, KC, 512])
            if not ada1_done:
                ada_layer(1)
                ada1_done = True
            if stage >= 2 and (not sample or stage >= 3):
                if bi == 0:
                    mamba_consts()
                if bi == 0 or sample:
                    mamba_masks(sample)
                rr = new_phase(["nt0", "nt1"] + ["nsq%d" % k for k in range(KC)])
                tmp = carve("ntmp", 0, [128, 2, 512], F32)
                sq_buf = carve("nsq", 4096, [128, KC, 512], BF16)
                norm_mod(T, 1, 0, sample, sq_buf, rr[2:10], tmp, rr[0:2])
                if sample:
                    mamba(tiles, 0, True, False)
                else:
                    mamba(tiles[0:2], 0, False, False)
                    mamba(tiles[2:4], 256, False, bi == 3)
                    if bi == 3:
                        mamba_prompt_out()
            mlp(T, 1, sample)
            if bi == 0:
                dump("x3", x_fm[:, :, :], R_x, [128, KC, 512])
            final_out(tiles)

        if not kb.dry:
            for ds in kb.out_sems:
                nc.gpsimd.wait_ge(ds.handle, ds.total)

    kb.dry = True
    emit()
    kb.dry = False
    ws.start_real()
    emit()
    return nc


def _slabs_b(W, kc, mg):
    K, M = W.shape
    mc = M // 128
    a = W.reshape(kc, 128, mc // mg, mg, 128).transpose(2, 1, 3, 0, 4)
    return np.ascontiguousarray(a).reshape(mc // mg, 128, mg * kc * 128)


def _slabs_a(W, kc, nb=512):
    K, N = W.shape
    a = W.reshape(kc, 128, N // nb, nb).transpose(2, 1, 0, 3)
    return np.ascontiguousarray(a).reshape(N // nb, 128, kc * nb)


def _fm(v):
    return np.ascontiguousarray(v.reshape(-1, 128).T)


def _host_consts():
    k = np.arange(128)
    same = (k[:, None] // LS) == (k[None, :] // LS)
    lt = (k[:, None] <= k[None, :])
    u = (k[:, None] > k[None, :])
    cst = np.zeros((128, C_N), np.float32)
    cst[:, C_ID:C_ID + 128] = np.eye(128)
    cst[:, C_LT:C_LT + 128] = lt
    cst[:, C_U:C_U + 128] = u
    cst[:, C_LTB:C_LTB + 128] = lt & same
    cst[:, C_UB:C_UB + 128] = u & same
    cst[:, C_SAME:C_SAME + 128] = same
    cst[:, C_ONES:C_ONES + 128] = 1.0
    cst[:, C_KILL:C_KILL + 128] = ~lt
    cst[:, C_KILLB:C_KILLB + 128] = ~(lt & same)
    cst[:, C_NEGI:C_NEGI + 128] = -30000.0 * np.eye(128)
    cst[:, C_SEQIND:C_SEQIND + 16] = (k[:, None] // LS) == np.arange(16)[None, :]
    seqm = ((k[None, :] // LS) == np.arange(16)[:, None]).astype(np.float32).reshape(1, 16 * 128)
    return cst, seqm


_PROGRAM = None
_LAST = None


def kernel(**inputs):
    global _PROGRAM
    f32 = np.float32
    inp = {k: np.asarray(v) for k, v in inputs.items()}
    slabs = [
        _slabs_a(inp["gm_w_in"][0][:, DGM:], KC),
        _slabs_b(inp["gm_w_in"][0][:, :DGM], KC, 4),
        _slabs_b(inp["gm_w_out"][0], 16, 2),
        _slabs_b(inp["mlp_w1"][0], KC, 4),
        _slabs_b(inp["mlp_w2"][0], 32, 1),
        _slabs_b(inp["ssm_w_in"][0][:, :6144], KC, 4),
        _slabs_b(inp["ssm_w_out"][0], 16, 2),
        _slabs_b(inp["mlp_w1"][1], KC, 4),
        _slabs_b(inp["mlp_w2"][1], 32, 1),
    ]
    w_all = np.ascontiguousarray(np.concatenate(slabs, axis=0).astype(f32))
    assert w_all.shape == (NSLAB, 128, 4096)
    w_ada = np.ascontiguousarray(np.concatenate([_slabs_a(inp["ada_w"][l], KC, 256) for l in range(2)], axis=0).astype(f32))
    pfm = np.zeros((128, PF_N), f32)
    for l in range(2):
        pfm[:, PF_ADAB + 48 * l:PF_ADAB + 48 * (l + 1)] = _fm(inp["ada_b"][l])
        pfm[:, PF_N1G + 8 * l:PF_N1G + 8 * (l + 1)] = _fm(inp["norm1_g"][l])
        pfm[:, PF_N2G + 8 * l:PF_N2G + 8 * (l + 1)] = _fm(inp["norm2_g"][l])
    pfm[:, PF_FG:PF_FG + 8] = _fm(inp["final_g"])
    pfm[:, PF_LNG:PF_LNG + 16] = _fm(inp["gm_ln_g"][0])
    pfm[:, PF_LNB:PF_LNB + 16] = _fm(inp["gm_ln_b"][0])
    for kk in range(4):
        pfm[:, PF_CW + 32 * kk:PF_CW + 32 * (kk + 1)] = _fm(inp["ssm_conv_w"][0][kk])
    pfm[:, PF_CB:PF_CB + 32] = _fm(inp["ssm_conv_b"][0])
    pfm[:, PF_SNG:PF_SNG + 16] = _fm(inp["ssm_norm_g"][0])
    pfm[:, PF_D:PF_D + 16] = _fm(np.repeat(inp["ssm_d"][0], HD))
    p32 = np.stack([inp["ssm_dt_bias"][0], inp["ssm_a_log"][0]], axis=1).astype(f32)
    cst, seqm = _host_consts()
    lngb = np.stack([inp["gm_ln_g"][0], inp["gm_ln_b"][0]], axis=0).astype(f32)
    ws_ = inp["gm_w_s"][0]
    wst_p = np.transpose(ws_, (2, 0, 1)).reshape(128, NG * 128)
    ws8 = np.tile(ws_[:, :LS, :LS], (1, NSAMP, NSAMP))
    wst_s = np.transpose(ws8, (2, 0, 1)).reshape(128, NG * 128)
    wst = np.ascontiguousarray(np.stack([wst_p, wst_s], axis=0).astype(f32))
    bs_p = inp["gm_b_s"][0].reshape(NG * 128)
    bs_s = np.tile(inp["gm_b_s"][0][:, :LS], (1, NSAMP)).reshape(NG * 128)
    bs = np.ascontiguousarray(np.stack([bs_p, bs_s], axis=0).astype(f32))
    wdt = np.ascontiguousarray(inp["ssm_w_in"][0][:, 6144:6176].reshape(KC, 128, NH).transpose(1, 0, 2).reshape(128, KC * NH).astype(f32))

    in_maps = []
    for c in range(NCORES):
        xin = np.concatenate([inp["x_prompt"][c], inp["x_sample"][c * NSAMP:(c + 1) * NSAMP].reshape(NSAMP * LS, D)], axis=0)
        cin = np.concatenate([inp["c_prompt"][c:c + 1], inp["c_sample"][c * NSAMP:(c + 1) * NSAMP]], axis=0)
        in_maps.append({
            "xin": np.ascontiguousarray(xin, dtype=f32), "cin": np.ascontiguousarray(cin, dtype=f32),
            "sst": np.ascontiguousarray(inp["state_ssm"][0, c * NSAMP:(c + 1) * NSAMP], dtype=f32),
            "scv": np.ascontiguousarray(inp["state_conv"][0, c * NSAMP:(c + 1) * NSAMP].reshape(NSAMP * 3, CONVD), dtype=f32),
            "w_all": w_all, "w_ada": w_ada, "pfm": pfm, "p32": p32, "cst": cst, "lngb": lngb, "wst": wst, "bs": bs,
            "wdt": wdt, "seqm": seqm,
        })
    if _PROGRAM is None:
        _PROGRAM = build_program()
    res = run_bass_kernel_spmd(_PROGRAM, in_maps, core_ids=list(range(NCORES)))
    R = res.results
    global _LAST
    _LAST = R
    y_prompt = np.stack([R[c]["y_out"][:SEQ] for c in range(NCORES)], axis=0)
    y_sample = np.concatenate([R[c]["y_out"][SEQ:].reshape(NSAMP, LS, D) for c in range(NCORES)], axis=0)
    gm_v_prompt = np.stack([R[c]["gmv_p"] for c in range(NCORES)], axis=0)[None]
    gm_v_sample = np.concatenate([R[c]["gmv_s"].reshape(NSAMP, LS, DGM) for c in range(NCORES)], axis=0)[None]
    ssm_p = np.stack([R[c]["ssm_p"].reshape(NH, HD, DST) for c in range(NCORES)], axis=0)[None]
    cv_p = np.stack([R[c]["cv_p"] for c in range(NCORES)], axis=0)[None]
    ssm_s = np.concatenate([R[c]["ssm_s"].reshape(NSAMP, NH, HD, DST) for c in range(NCORES)], axis=0)[None]
    cv_s = np.concatenate([R[c]["cv_s"] for c in range(NCORES)], axis=0)[None]
    return tuple(np.ascontiguousarray(a, dtype=f32) for a in
                 (y_prompt, y_sample, gm_v_prompt, gm_v_sample, ssm_p, cv_p, ssm_s, cv_s))
```
